# Optimizing a Trainium2 kernel written in Bass

```python
import jax, jax.numpy as jnp
from jax import lax
import numpy as np

D_MODEL = 1024
BATCH = 1
SEQ = 16384
DEPTH = 2
DEC_BATCH = 8
DEC_SEQ = 64
PAST_LEN = 2048

CHUNK = 64
Q_BLOCK = 128
H_A = 4
DH_A = 64
D_A = H_A * DH_A
H_B = 4
DH_B = 64
D_B = H_B * DH_B
H_C = 4
DH_C = 128
D_C = H_C * DH_C
D_MIX = D_A + D_B + D_C
GDN_CONV = 4
D_FF = 2816
FFN_CONV = 3
IN_COLS = 3 * D_A + 4 * D_B + 2 * H_B + 4 * D_C + 2 * H_C
EPS = 1e-6
NEG = -1e30

kernel_name = 'hymba_sb_gdn_mlstm_stream_step'


def _rmsnorm(x, g):
    xf = x.astype(jnp.float32)
    y = xf * lax.rsqrt(jnp.mean(xf * xf, axis=-1, keepdims=True) + EPS)
    return (y * g.astype(jnp.float32)).astype(x.dtype)


def _l2norm(x):
    xf = x.astype(jnp.float32)
    return xf * lax.rsqrt(jnp.sum(xf * xf, axis=-1, keepdims=True) + EPS)


def _heads(x, h):
    b, t, _ = x.shape
    return x.reshape(b, t, h, -1).transpose(0, 2, 1, 3)


def _merge(x):
    b, h, t, d = x.shape
    return x.transpose(0, 2, 1, 3).reshape(b, t, h * d)


def _to_chunks(x, L):
    b, h, t = x.shape[:3]
    return jnp.moveaxis(x.reshape((b, h, t // L, L) + x.shape[3:]), 2, 0)


def _from_chunks(x):
    n, b, h, L = x.shape[:4]
    return jnp.moveaxis(x, 0, 2).reshape((b, h, n * L) + x.shape[4:])


def _causal_dwconv(x, hist, w):
    width, t = w.shape[0], x.shape[1]
    xp = jnp.concatenate([hist.astype(x.dtype), x], axis=1)
    y = xp[:, 0:t] * w[0]
    for j in range(1, width):
        y = y + xp[:, j:j + t] * w[j]
    return y, xp[:, t:]


def _split_cols(proj):
    sizes = [D_A, D_A, D_A, 3 * D_B, D_B, H_B, H_B, D_C, D_C, D_C, D_C, H_C, H_C]
    return jnp.split(proj, np.cumsum(sizes)[:-1].tolist(), axis=-1)


def _stick_breaking(q, k, v, q_start):
    b, h, t, d = q.shape
    tk = k.shape[2]
    qb = min(Q_BLOCK, t)
    nb = t // qb
    kf = k.astype(jnp.float32)
    vf = v.astype(jnp.float32)
    k_pos = jnp.arange(tk)
    qs = jnp.moveaxis((q.astype(jnp.float32) * d ** -0.5).reshape(b, h, nb, qb, d), 2, 0)

    def block(args):
        qblk, i = args
        q_pos = q_start + i * qb + jnp.arange(qb)
        z = jnp.einsum('bhqd,bhkd->bhqk', qblk, kf)
        before = k_pos[None, :] < q_pos[:, None]
        sp = jnp.where(before, jax.nn.softplus(z), 0.0)
        rest = lax.cumsum(sp, axis=3, reverse=True) - sp
        log_a = jnp.where(before, jax.nn.log_sigmoid(z) - rest, -jnp.inf)
        return jnp.einsum('bhqk,bhkd->bhqd', jnp.exp(log_a), vf)

    o = lax.map(block, (qs, jnp.arange(nb)))
    return jnp.moveaxis(o, 0, 2).reshape(b, h, t, d)


def _gdn_chunk(s, inp):
    q, k, v, beta, g = inp
    L = q.shape[2]
    dv = v.shape[-1]
    tri = jnp.tril(jnp.ones((L, L), bool))
    strict = jnp.tril(jnp.ones((L, L), bool), -1)
    gc = jnp.cumsum(g, axis=-1)
    gam = jnp.exp(jnp.where(tri, gc[..., :, None] - gc[..., None, :], -jnp.inf))
    m = jnp.where(strict, beta[..., :, None] * jnp.einsum('bhtd,bhsd->bhts', k, k) * gam, 0.0)
    rhs = jnp.concatenate([beta[..., None] * v, (beta * jnp.exp(gc))[..., None] * k], axis=-1)
    sol = lax.linalg.triangular_solve(m, rhs, left_side=True, lower=True, unit_diagonal=True)
    u, w = sol[..., :dv], sol[..., dv:]
    v_new = u - jnp.einsum('bhtk,bhkv->bhtv', w, s)
    qk = jnp.where(tri, jnp.einsum('bhtd,bhsd->bhts', q, k) * gam, 0.0)
    o = (jnp.einsum('bhtk,bhkv->bhtv', q * jnp.exp(gc)[..., None], s)
         + jnp.einsum('bhts,bhsv->bhtv', qk, v_new))
    gl = gc[..., -1:]
    s_new = (s * jnp.exp(gl)[..., None]
             + jnp.einsum('bhs,bhsk,bhsv->bhkv', jnp.exp(gl - gc), k, v_new))
    return s_new, o


def _mlstm_chunk(carry, inp):
    c, n, m0 = carry
    q, k, v, ig, lf = inp
    L = q.shape[2]
    tri = jnp.tril(jnp.ones((L, L), bool))
    bcum = jnp.cumsum(lf, axis=-1)
    dmat = jnp.where(tri, bcum[..., :, None] - bcum[..., None, :] + ig[..., None, :], -jnp.inf)
    g = bcum + m0[..., None]
    m = jnp.maximum(g, jnp.max(dmat, axis=-1))
    w = jnp.exp(dmat - m[..., None])
    inter = jnp.exp(g - m)
    qk = jnp.einsum('bhtd,bhsd->bhts', q, k) * w
    num = (inter[..., None] * jnp.einsum('bhtk,bhkv->bhtv', q, c)
           + jnp.einsum('bhts,bhsv->bhtv', qk, v))
    den = inter * jnp.einsum('bhtk,bhk->bht', q, n) + jnp.sum(qk, axis=-1)
    h = num / jnp.maximum(jnp.abs(den), jnp.exp(-m))[..., None]
    m_last = m[..., -1]
    decay = jnp.exp(g[..., -1] - m_last)
    wk = jnp.exp(bcum[..., -1:] - bcum + ig - m_last[..., None])
    c_new = decay[..., None, None] * c + jnp.einsum('bhs,bhsk,bhsv->bhkv', wk, k, v)
    n_new = decay[..., None] * n + jnp.einsum('bhs,bhsk->bhk', wk, k)
    return (c_new, n_new, m_last), h


def _layer(x, kv_k, kv_v, gdn_hist, gdn_s, m_c, m_n, m_m, ffn_hist,
           g_mix_pre, g_mix_post, g_ffn_pre, g_ffn_post, w_in, gdn_conv_w, gdn_a_log,
           gdn_dt_bias, gdn_norm_g, mlstm_b_i, mlstm_b_f, mlstm_norm_g, w_out,
           ffn_w_up, ffn_conv_w, ffn_w_down):
    dt = x.dtype
    f32 = jnp.float32
    t = x.shape[1]
    L = min(CHUNK, t)
    h = _rmsnorm(x, g_mix_pre)
    (q_a, k_a, v_a, qkv_b, z_b, beta_b, a_b,
     q_c, k_c, v_c, o_c, i_c, f_c) = _split_cols(h @ w_in)

    q_a, k_a, v_a = _heads(q_a, H_A), _heads(k_a, H_A), _heads(v_a, H_A)
    k_all = jnp.concatenate([kv_k.astype(dt), k_a], axis=2)
    v_all = jnp.concatenate([kv_v.astype(dt), v_a], axis=2)
    out_a = _merge(_stick_breaking(q_a, k_all, v_all, kv_k.shape[2]).astype(dt))

    conv_b, gdn_hist_new = _causal_dwconv(qkv_b, gdn_hist, gdn_conv_w)
    q_b, k_b, v_b = jnp.split(jax.nn.silu(conv_b), 3, axis=-1)
    q_b = _l2norm(_heads(q_b, H_B)) * DH_B ** -0.5
    k_b = _l2norm(_heads(k_b, H_B))
    v_b = _heads(v_b, H_B).astype(f32)
    beta = jax.nn.sigmoid(beta_b.astype(f32)).transpose(0, 2, 1)
    g_log = (-jnp.exp(gdn_a_log.astype(f32))
             * jax.nn.softplus(a_b.astype(f32) + gdn_dt_bias.astype(f32))).transpose(0, 2, 1)
    xs_b = (_to_chunks(q_b, L), _to_chunks(k_b, L), _to_chunks(v_b, L),
            _to_chunks(beta, L), _to_chunks(g_log, L))
    s_new, o_b = lax.scan(_gdn_chunk, gdn_s.astype(f32), xs_b)
    o_b = _rmsnorm(_from_chunks(o_b), gdn_norm_g) * jax.nn.silu(_heads(z_b, H_B).astype(f32))
    out_b = _merge(o_b.astype(dt))

    q_c = _heads(q_c, H_C).astype(f32)
    k_c = _heads(k_c, H_C).astype(f32) * DH_C ** -0.5
    v_c = _heads(v_c, H_C).astype(f32)
    ig = (i_c.astype(f32) + mlstm_b_i.astype(f32)).transpose(0, 2, 1)
    lf = jax.nn.log_sigmoid(f_c.astype(f32) + mlstm_b_f.astype(f32)).transpose(0, 2, 1)
    xs_c = (_to_chunks(q_c, L), _to_chunks(k_c, L), _to_chunks(v_c, L),
            _to_chunks(ig, L), _to_chunks(lf, L))
    (c_new, n_new, m_new), h_c = lax.scan(
        _mlstm_chunk, (m_c.astype(f32), m_n.astype(f32), m_m.astype(f32)), xs_c)
    h_c = _rmsnorm(_from_chunks(h_c), mlstm_norm_g) * jax.nn.sigmoid(_heads(o_c, H_C).astype(f32))
    out_c = _merge(h_c.astype(dt))

    mix = jnp.concatenate([out_a, out_b, out_c], axis=-1) @ w_out
    x = x + _rmsnorm(mix, g_mix_post)

    h = _rmsnorm(x, g_ffn_pre)
    gate, up = jnp.split(h @ ffn_w_up, 2, axis=-1)
    gate, ffn_hist_new = _causal_dwconv(gate, ffn_hist, ffn_conv_w)
    y = (jax.nn.gelu(gate, approximate=True) * up) @ ffn_w_down
    x = x + _rmsnorm(y, g_ffn_post)
    return (x, k_a, v_a, gdn_hist_new.astype(dt), s_new.astype(dt), c_new.astype(dt),
            n_new.astype(dt), m_new.astype(dt), ffn_hist_new.astype(dt))


def setup_inputs(seed: int = 0) -> dict:
    key = jax.random.key(seed)
    ks = jax.random.split(key, 32)

    def nrm(k, shape, s=1.0):
        return s * jax.random.normal(k, shape, jnp.float32)

    dt_init = jnp.exp(jax.random.uniform(ks[20], (DEPTH, H_B), jnp.float32,
                                         np.log(1e-3), np.log(1e-1)))
    return {
        'x_prompt': nrm(ks[0], (BATCH, SEQ, D_MODEL)),
        'x_sample': nrm(ks[1], (DEC_BATCH, DEC_SEQ, D_MODEL)),
        'cache_sb_k': nrm(ks[2], (DEPTH, DEC_BATCH, H_A, PAST_LEN, DH_A)),
        'cache_sb_v': nrm(ks[3], (DEPTH, DEC_BATCH, H_A, PAST_LEN, DH_A)),
        'state_gdn_conv': nrm(ks[4], (DEPTH, DEC_BATCH, GDN_CONV - 1, 3 * D_B)),
        'state_gdn_s': nrm(ks[5], (DEPTH, DEC_BATCH, H_B, DH_B, DH_B), 0.1),
        'state_mlstm_c': nrm(ks[6], (DEPTH, DEC_BATCH, H_C, DH_C, DH_C), 0.1),
        'state_mlstm_n': nrm(ks[7], (DEPTH, DEC_BATCH, H_C, DH_C), 0.1),
        'state_mlstm_m': nrm(ks[8], (DEPTH, DEC_BATCH, H_C)),
        'state_ffn_conv': nrm(ks[9], (DEPTH, DEC_BATCH, FFN_CONV - 1, D_FF)),
        'g_mix_pre': 1.0 + nrm(ks[10], (DEPTH, D_MODEL), 0.02),
        'g_mix_post': 1.0 + nrm(ks[11], (DEPTH, D_MODEL), 0.02),
        'g_ffn_pre': 1.0 + nrm(ks[12], (DEPTH, D_MODEL), 0.02),
        'g_ffn_post': 1.0 + nrm(ks[13], (DEPTH, D_MODEL), 0.02),
        'w_in': nrm(ks[14], (DEPTH, D_MODEL, IN_COLS), D_MODEL ** -0.5),
        'gdn_conv_w': nrm(ks[15], (DEPTH, GDN_CONV, 3 * D_B), GDN_CONV ** -0.5),
        'gdn_a_log': jnp.log(jax.random.uniform(ks[16], (DEPTH, H_B), jnp.float32, 1.0, 16.0)),
        'gdn_dt_bias': dt_init + jnp.log(-jnp.expm1(-dt_init)),
        'gdn_norm_g': 1.0 + nrm(ks[17], (DEPTH, DH_B), 0.02),
        'mlstm_b_i': nrm(ks[18], (DEPTH, H_C), 0.1),
        'mlstm_b_f': jnp.linspace(3.0, 6.0, H_C)[None, :] + nrm(ks[19], (DEPTH, H_C), 0.1),
        'mlstm_norm_g': 1.0 + nrm(ks[21], (DEPTH, DH_C), 0.02),
        'w_out': nrm(ks[22], (DEPTH, D_MIX, D_MODEL), D_MIX ** -0.5),
        'ffn_w_up': nrm(ks[23], (DEPTH, D_MODEL, 2 * D_FF), D_MODEL ** -0.5),
        'ffn_conv_w': nrm(ks[24], (DEPTH, FFN_CONV, D_FF), FFN_CONV ** -0.5),
        'ffn_w_down': nrm(ks[25], (DEPTH, D_FF, D_MODEL), D_FF ** -0.5),
    }


def reference(x_prompt, x_sample, cache_sb_k, cache_sb_v, state_gdn_conv, state_gdn_s,
              state_mlstm_c, state_mlstm_n, state_mlstm_m, state_ffn_conv,
              g_mix_pre, g_mix_post, g_ffn_pre, g_ffn_post, w_in, gdn_conv_w, gdn_a_log,
              gdn_dt_bias, gdn_norm_g, mlstm_b_i, mlstm_b_f, mlstm_norm_g, w_out,
              ffn_w_up, ffn_conv_w, ffn_w_down):
    pdt = x_prompt.dtype
    b = x_prompt.shape[0]
    f32 = jnp.float32
    empty_kv = jnp.zeros((b, H_A, 0, DH_A), pdt)
    zero_gdn_hist = jnp.zeros((b, GDN_CONV - 1, 3 * D_B), pdt)
    zero_s = jnp.zeros((b, H_B, DH_B, DH_B), f32)
    zero_c = jnp.zeros((b, H_C, DH_C, DH_C), f32)
    zero_n = jnp.zeros((b, H_C, DH_C), f32)
    m_init = jnp.full((b, H_C), NEG, f32)
    zero_ffn_hist = jnp.zeros((b, FFN_CONV - 1, D_FF), pdt)

    xp, xs = x_prompt, x_sample
    new_p, new_s = [], []
    for l in range(DEPTH):
        lw = (g_mix_pre[l], g_mix_post[l], g_ffn_pre[l], g_ffn_post[l], w_in[l], gdn_conv_w[l],
              gdn_a_log[l], gdn_dt_bias[l], gdn_norm_g[l], mlstm_b_i[l], mlstm_b_f[l],
              mlstm_norm_g[l], w_out[l], ffn_w_up[l], ffn_conv_w[l], ffn_w_down[l])
        xp, *st_p = _layer(xp, empty_kv, empty_kv, zero_gdn_hist, zero_s, zero_c, zero_n,
                           m_init, zero_ffn_hist, *lw)
        xs, *st_s = _layer(xs, cache_sb_k[l], cache_sb_v[l], state_gdn_conv[l], state_gdn_s[l],
                           state_mlstm_c[l], state_mlstm_n[l], state_mlstm_m[l],
                           state_ffn_conv[l], *lw)
        new_p.append(st_p)
        new_s.append(st_s)
    (sb_k_p, sb_v_p, gdn_conv_p, gdn_s_p, mlstm_c_p, mlstm_n_p, mlstm_m_p,
     ffn_conv_p) = [jnp.stack(a) for a in zip(*new_p)]
    (sb_k_s, sb_v_s, gdn_conv_s, gdn_s_s, mlstm_c_s, mlstm_n_s, mlstm_m_s,
     ffn_conv_s) = [jnp.stack(a) for a in zip(*new_s)]
    return (xp, xs,
            sb_k_p, sb_v_p, gdn_conv_p, gdn_s_p, mlstm_c_p, mlstm_n_p, mlstm_m_p, ffn_conv_p,
            sb_k_s, sb_v_s, gdn_conv_s, gdn_s_s, mlstm_c_s, mlstm_n_s, mlstm_m_s, ffn_conv_s)
```

```python
import contextlib
import numpy as np
import concourse.bass as bass
import concourse.mybir as mybir
from concourse.bass_utils import run_bass_kernel_spmd

F32 = mybir.dt.float32
BF16 = mybir.dt.bfloat16
AF = mybir.ActivationFunctionType
ALU = mybir.AluOpType
AX = mybir.AxisListType

N_DMA_SEMS = 24
KLIM = 99
SEM_EPOCH = 20000
KSUB = 99
KHH = 0
D = 1024
DFF = 2816
NCOL = 3856
EPS = 1e-6
NEG = -1e30


class _Cut(Exception):
    pass


class Sched:
    def __init__(self, nc, stack):
        self.nc = nc
        self.engs = {"pe": nc.tensor, "act": nc.scalar, "dve": nc.vector, "pool": nc.gpsimd, "sp": nc.sync}
        self.sems = {}
        self.cnt = {}
        for e in ("pe", "act", "dve", "pool"):
            self.sems[e] = stack.enter_context(nc.semaphore("s_" + e))
            self.cnt[e] = 0
        for i in range(N_DMA_SEMS):
            self.sems[("d", i)] = stack.enter_context(nc.semaphore("s_dma%d" % i))
            self.cnt[("d", i)] = 0
        self.dnext = 0
        self.stack = stack
        self.epoch = 0
        self.waited = {e: {} for e in self.engs}
        self.lastw = {}
        self.reads = {}
        self.n_inst = 0

    def _wait(self, e, evs):
        best = {}
        for ev in evs:
            if ev is None:
                continue
            k, v = ev
            if k == "pe" and e == "pe":
                continue
            if self.waited[e].get(k, 0) < v and best.get(k, 0) < v:
                best[k] = v
        for k, v in best.items():
            self.engs[e].wait_ge(self.sems[k], v)
            self.waited[e][k] = v

    def _deps(self, reads, writes):
        evs = []
        for r in reads:
            evs.append(self.lastw.get(r))
        for w in writes:
            evs.append(self.lastw.get(w))
            evs.extend(self.reads.get(w, ()))
        return evs

    def _commit(self, ev, reads, writes):
        for r in reads:
            lst = self.reads.setdefault(r, [])
            lst.append(ev)
            if len(lst) > 16:
                d = {}
                for k, v in lst:
                    if d.get(k, 0) < v:
                        d[k] = v
                self.reads[r] = list(d.items())
        for w in writes:
            self.lastw[w] = ev
            self.reads[w] = []

    def maybe_epoch(self):
        if max(self.cnt[e] for e in ("pe", "act", "dve", "pool")) < SEM_EPOCH:
            return
        evs = [(e, self.cnt[e]) for e in ("pe", "act", "dve", "pool") if self.cnt[e] > 0]
        for e in self.engs:
            self._wait(e, evs)
        self.epoch += 1
        for e in ("pe", "act", "dve", "pool"):
            self.sems[e] = self.stack.enter_context(self.nc.semaphore("s_%s_%d" % (e, self.epoch)))
            self.cnt[e] = 0
        comp = ("pe", "act", "dve", "pool")
        for e in self.engs:
            self.waited[e] = {k: v for k, v in self.waited[e].items() if k not in comp}
        self.lastw = {r: ev for r, ev in self.lastw.items() if ev[0] not in comp}
        self.reads = {r: [ev for ev in lst if ev[0] not in comp] for r, lst in self.reads.items()}

    def op(self, e, fn, reads=(), writes=()):
        self.maybe_epoch()
        self._wait(e, self._deps(reads, writes))
        ins = fn(self.engs[e])
        self.cnt[e] += 1
        ins.then_inc(self.sems[e], 1)
        ev = (e, self.cnt[e])
        self._commit(ev, reads, writes)
        self.n_inst += 1
        return ev

    def dma(self, q, out, in_, reads=(), writes=(), **kw):
        i = self.dnext
        self.dnext = (self.dnext + 1) % N_DMA_SEMS
        k = ("d", i)
        evs = self._deps(reads, writes)
        if self.cnt[k] > 0:
            evs.append((k, self.cnt[k]))
        self._wait(q, evs)
        ins = self.engs[q].dma_start(out=out, in_=in_, **kw)
        self.cnt[k] += 16
        ins.then_inc(self.sems[k], 16)
        ev = (k, self.cnt[k])
        self._commit(ev, reads, writes)
        self.n_inst += 1
        return ev

    def barrier(self):
        evs = [(k, c) for k, c in self.cnt.items() if c > 0]
        for e in self.engs:
            self._wait(e, evs)


def build(SEQ, PAST, NS=64):
    nc = bass.Bass("TRN2", target_bir_lowering=False)

    def din(name, shape):
        return nc.dram_tensor(name, list(shape), F32, kind="ExternalInput").ap()

    def dout(name, shape):
        return nc.dram_tensor(name, list(shape), F32, kind="ExternalOutput").ap()

    def dscr(name, shape, dt=F32):
        return nc.dram_tensor(name, list(shape), dt).ap()

    x_in = {"p": din("x_prompt", [SEQ, D]), "s": din("x_sample", [NS, D])}
    cache_k = din("cache_sb_k", [2, 4, PAST, 64])
    cache_v = din("cache_sb_v", [2, 4, PAST, 64])
    st_gconv = din("state_gdn_conv", [2, 3, 768])
    st_gs = din("state_gdn_s", [2, 4, 64, 64])
    st_mc = din("state_mlstm_c", [2, 4, 128, 128])
    st_mn = din("state_mlstm_n", [2, 4, 128])
    st_mm = din("state_mlstm_m", [2, 4])
    st_fconv = din("state_ffn_conv", [2, 2, DFF])
    g_mix_pre = din("g_mix_pre", [2, D]); g_mix_post = din("g_mix_post", [2, D])
    g_ffn_pre = din("g_ffn_pre", [2, D]); g_ffn_post = din("g_ffn_post", [2, D])
    w_in = din("w_in", [2, D, NCOL])
    gdn_conv_w = din("gdn_conv_w", [2, 4, 768])
    gdn_a_log = din("gdn_a_log", [2, 4]); gdn_dt_bias = din("gdn_dt_bias", [2, 4])
    gdn_norm_g = din("gdn_norm_g", [2, 64])
    mlstm_b_i = din("mlstm_b_i", [2, 4]); mlstm_b_f = din("mlstm_b_f", [2, 4])
    mlstm_norm_g = din("mlstm_norm_g", [2, 128])
    w_out = din("w_out", [2, D, D])
    ffn_w_up = din("ffn_w_up", [2, D, 2 * DFF])
    ffn_conv_w = din("ffn_conv_w", [2, 3, DFF])
    ffn_w_down = din("ffn_w_down", [2, DFF, D])

    cmask_in = din("cmask", [12, 64, 64])
    TG = {"p": SEQ, "s": NS}
    PG = {"p": 0, "s": PAST}
    y_out = {"p": dout("y_prompt", [SEQ, D]), "s": dout("y_sample", [NS, D])}
    o_sbk = {g: dout("sb_k_" + g, [2, 4, TG[g], 64]) for g in "ps"}
    o_sbv = {g: dout("sb_v_" + g, [2, 4, TG[g], 64]) for g in "ps"}
    o_gconv = {g: dout("gdn_conv_" + g, [2, 3, 768]) for g in "ps"}
    o_gs = {g: dout("gdn_s_" + g, [2, 4, 64, 64]) for g in "ps"}
    o_mc = {g: dout("mlstm_c_" + g, [2, 4, 128, 128]) for g in "ps"}
    o_mn = {g: dout("mlstm_n_" + g, [2, 4, 128]) for g in "ps"}
    o_mm = {g: dout("mlstm_m_" + g, [2, 4]) for g in "ps"}
    o_fconv = {g: dout("ffn_conv_" + g, [2, 2, DFF]) for g in "ps"}

    xT = {g: dscr("xT_" + g, [D, TG[g]]) for g in "ps"}
    x1T = {g: dscr("x1T_" + g, [D, TG[g]]) for g in "ps"}
    y2p = {g: dscr("y2p_" + g, [D, TG[g]]) for g in "ps"}
    PF = {g: dscr("PF_" + g, [1792, TG[g]]) for g in "ps"}
    PT = {g: dscr("PT_" + g, [TG[g], 2320]) for g in "ps"}
    QAT = {g: dscr("QAT_" + g, [256, TG[g]], BF16) for g in "ps"}
    KAT = {g: dscr("KAT_" + g, [256, PG[g] + TG[g]], BF16) for g in "ps"}
    VA = {g: dscr("VA_" + g, [PG[g] + TG[g], 256], BF16) for g in "ps"}
    mixT = {g: dscr("mixT_" + g, [D, TG[g]], BF16) for g in "ps"}

    with contextlib.ExitStack() as st:
        S = Sched(nc, st)

        uniq = [0]

        def sb(stack, name, shape, dt=F32):
            uniq[0] += 1
            return stack.enter_context(nc.sbuf_tensor("%s_%d" % (name, uniq[0]), list(shape), dt))

        ps = [st.enter_context(nc.psum_tensor("ps%d" % i, [128, 512], F32)) for i in range(8)]
        PSN = ["ps%d" % i for i in range(8)]
        rot = [0]

        def nextps(lo=0, hi=8):
            i = lo + rot[0] % (hi - lo)
            rot[0] += 1
            return i

        def mm(pi, out, lhsT, rhs, start, stop, reads):
            S.op("pe", lambda e: e.matmul(out, lhsT=lhsT, rhs=rhs, start=start, stop=stop), reads=reads, writes=[PSN[pi]])

        def tr(pi, out, in_, n, reads):
            S.op("pe", lambda e: e.transpose(out=out, in_=in_, identity=ident[:n, :n]), reads=list(reads) + ["ident"], writes=[PSN[pi]])

        def act(out, in_, func, reads, writes, bias=0.0, scale=1.0):
            S.op("act", lambda e: e.activation(out=out, in_=in_, func=func, bias=bias, scale=scale), reads=reads, writes=writes)

        def dve(fn, reads, writes):
            S.op("dve", fn, reads=reads, writes=writes)

        def pool(fn, reads, writes):
            S.op("pool", fn, reads=reads, writes=writes)

        def ld(out, in_, writes, q="sp", reads=(), **kw):
            S.dma(q, out, in_, reads=reads, writes=writes, **kw)

        def stg(out, in_, reads, q="sp", writes=(), **kw):
            S.dma(q, out, in_, reads=reads, writes=writes, **kw)

        ident = sb(st, "ident", [128, 128]); ones = sb(st, "ones", [128, 128])
        Uge = sb(st, "Uge", [128, 128])
        LT64 = sb(st, "LT64", [64, 64]); SL64 = sb(st, "SL64", [64, 64])
        UT64 = sb(st, "UT64", [64, 64]); SU64 = sb(st, "SU64", [64, 64])
        SELL = sb(st, "SELL", [64, 128])
        amask = sb(st, "amask", [128, 4, 512])
        pool(lambda e: e.memset(ones[:], 1.0), [], ["ones"])
        pool(lambda e: e.memset(amask[:], 1.0), [], ["amask"])

        def sel(out, in_, pattern, op, base, cm, reads, writes, fill=0.0):
            pool(lambda e: e.affine_select(out=out, in_=in_, pattern=pattern, compare_op=op, fill=fill, base=base, channel_multiplier=cm), reads, writes)

        sel(ident[:], ones[:], [[-1, 128]], ALU.is_equal, 0, 1, ["ones"], ["ident"])
        sel(Uge[:], ones[:], [[-1, 128]], ALU.is_ge, 0, 1, ["ones"], ["Uge"])
        sel(LT64[:], ones[:64, :64], [[-1, 64]], ALU.is_ge, 0, 1, ["ones"], ["LT64"])
        sel(SL64[:], ones[:64, :64], [[-1, 64]], ALU.is_ge, -1, 1, ["ones"], ["SL64"])
        sel(UT64[:], ones[:64, :64], [[1, 64]], ALU.is_ge, 0, -1, ["ones"], ["UT64"])
        sel(SU64[:], ones[:64, :64], [[1, 64]], ALU.is_ge, -1, -1, ["ones"], ["SU64"])
        sel(SELL[:], ones[:64, :], [[0, 128]], ALU.is_equal, -63, 1, ["ones"], ["SELL"])
        for j in range(4):
            sel(amask[:, j, :], amask[:, j, :], [[1, 512]], ALU.is_ge, -128 * j - 1, -1, ["amask"], ["amask"])

        gpre = sb(st, "gpre", [128, 8]); gpost = sb(st, "gpost", [128, 8])
        gfpre = sb(st, "gfpre", [128, 8]); gfpost = sb(st, "gfpost", [128, 8])
        wcg = sb(st, "wcg", [128, 6, 4]); wcf = sb(st, "wcf", [128, 22, 3])
        alog = sb(st, "alog", [128, 4]); dtb = sb(st, "dtb", [128, 4])
        bi_b = sb(st, "bi_b", [128, 4]); bf_b = sb(st, "bf_b", [128, 4])
        gng = sb(st, "gng", [128, 64]); mng = sb(st, "mng", [128, 128])
        nexpa = sb(st, "nexpa", [128, 4])

        def load_params(l):
            for t, src in ((gpre, g_mix_pre), (gpost, g_mix_post), (gfpre, g_ffn_pre), (gfpost, g_ffn_post)):
                ld(t[:], src[l].rearrange("(c p) -> p c", p=128), ["par"], allow_slow_non_contiguous=True)
            for j in range(4):
                ld(wcg[:, :, j], gdn_conv_w[l, j].rearrange("(b p) -> p b", p=128), ["par"], allow_slow_non_contiguous=True)
            for j in range(3):
                ld(wcf[:, :, j], ffn_conv_w[l, j].rearrange("(b p) -> p b", p=128), ["par"], allow_slow_non_contiguous=True)
            ld(alog[:], gdn_a_log[l].partition_broadcast(128), ["par"])
            ld(dtb[:], gdn_dt_bias[l].partition_broadcast(128), ["par"])
            ld(bi_b[:], mlstm_b_i[l].partition_broadcast(128), ["par"])
            ld(bf_b[:], mlstm_b_f[l].partition_broadcast(128), ["par"])
            ld(gng[:], gdn_norm_g[l].partition_broadcast(128), ["par"])
            ld(mng[:], mlstm_norm_g[l].partition_broadcast(128), ["par"])
            act(nexpa[:], alog[:], AF.Exp, ["par"], ["par2"])
            dve(lambda e: e.tensor_scalar(out=nexpa[:], in0=nexpa[:], scalar1=-1.0, scalar2=None, op0=ALU.mult), ["par2"], ["par2"])
            S.barrier()

        def fm_rstd(xt, sq, rstd, TT, xres, sqres, rres):
            act(sq[:, :, :TT], xt[:, :, :TT], AF.Square, [xres], [sqres])
            pi = nextps()
            for c in range(8):
                mm(pi, ps[pi][:, :TT], ones[:, :], sq[:, c, :TT], c == 0, c == 7, [sqres, "ones"])
            dve(lambda e: e.tensor_scalar(out=rstd[:, :TT], in0=ps[pi][:, :TT], scalar1=1.0 / D, scalar2=EPS, op0=ALU.mult, op1=ALU.add), [PSN[pi]], [rres])
            act(rstd[:, :TT], rstd[:, :TT], AF.Sqrt, [rres], [rres])
            dve(lambda e: e.reciprocal(out=rstd[:, :TT], in_=rstd[:, :TT]), [rres], [rres])

        def tiles_of(T, TTmax):
            TT = min(TTmax, T)
            return [(t0, TT) for t0 in range(0, T, TT)]

        def phase_T0():
            with contextlib.ExitStack() as ph:
                xin = [sb(ph, "t0x%d" % i, [128, D]) for i in range(2)]
                xo = [sb(ph, "t0o%d" % i, [128, 8, 128]) for i in range(2)]
                it = 0
                for g in "ps":
                    T = TG[g]
                    NT = min(128, T)
                    for j in range(T // NT):
                        b = it % 2; it += 1
                        ld(xin[b][:NT, :], x_in[g][j * NT:(j + 1) * NT, :], ["t0x%d" % b])
                        for c in range(8):
                            pi = 4 * b + c // 4
                            tr(pi, ps[pi][:, (c % 4) * 128:(c % 4) * 128 + NT], xin[b][:NT, c * 128:(c + 1) * 128], NT, ["t0x%d" % b])
                        for h in range(2):
                            pi = 4 * b + h
                            act(xo[b][:, 4 * h:4 * h + 4, :NT], ps[pi][:, :].rearrange("p (c t) -> p c t", t=128)[:, :, :NT], AF.Copy, [PSN[pi]], ["t0o%d" % b])
                        stg(xT[g].rearrange("(c p) t -> p c t", p=128)[:, :, j * NT:(j + 1) * NT], xo[b][:, :, :NT], ["t0o%d" % b])
                S.barrier()

        FBLK = [768 + 128 * i for i in range(6)] + [1800 + 128 * i for i in range(4)] + [2312 + 128 * i for i in range(4)]
        TGRP = [(256, 512, 0), (1536, 264, 512), (3848, 8, 776), (2312, 512, 784), (2824, 512, 1296), (3336, 512, 1808)]

        def phase_A(l):
            with contextlib.ExitStack() as ph:
                Win = sb(ph, "Win", [128, 8, NCOL], BF16)
                for c in range(8):
                    ld(Win[:, c, :], w_in[l, c * 128:(c + 1) * 128, :], ["Win"], q="pool")
                xt = sb(ph, "Axt", [128, 8, 512]); sq = sb(ph, "Asq", [128, 8, 512])
                rstd = sb(ph, "Arstd", [128, 512]); hT = sb(ph, "AhT", [128, 8, 512], BF16)
                PFt = sb(ph, "APFt", [128, 14, 512]); QKt = sb(ph, "AQKt", [128, 4, 512], BF16)
                PTt = sb(ph, "APTt", [128, 2320]); VAt = sb(ph, "AVAt", [128, 256], BF16)
                for g in "ps":
                    T = TG[g]
                    for (t0, TT) in tiles_of(T, 512):
                        ld(xt[:, :, :TT], xT[g].rearrange("(c p) t -> p c t", p=128)[:, :, t0:t0 + TT], ["Axt"])
                        fm_rstd(xt, sq, rstd, TT, "Axt", "Asq", "Arstd")
                        for c in range(8):
                            dve(lambda e: e.scalar_tensor_tensor(out=hT[:, c, :TT], in0=xt[:, c, :TT], scalar=gpre[:, c:c + 1], in1=rstd[:, :TT], op0=ALU.mult, op1=ALU.mult), ["Axt", "Arstd", "par"], ["AhT"])
                        for i in range(4):
                            pi = nextps()
                            for c in range(8):
                                mm(pi, ps[pi][:, :TT], Win[:, c, 128 * i:128 * i + 128], hT[:, c, :TT], c == 0, c == 7, ["Win", "AhT"])
                            act(QKt[:, i, :TT], ps[pi][:, :TT], AF.Copy, [PSN[pi]], ["AQKt"], scale=0.125 if i < 2 else 1.0)
                        stg(QAT[g].rearrange("(c p) t -> p c t", p=128)[:, :, t0:t0 + TT], QKt[:, 0:2, :TT], ["AQKt"])
                        stg(KAT[g].rearrange("(c p) t -> p c t", p=128)[:, :, PG[g] + t0:PG[g] + t0 + TT], QKt[:, 2:4, :TT], ["AQKt"])
                        for i, off in enumerate(FBLK):
                            pi = nextps()
                            for c in range(8):
                                mm(pi, ps[pi][:, :TT], Win[:, c, off:off + 128], hT[:, c, :TT], c == 0, c == 7, ["Win", "AhT"])
                            if i % 2 == 0:
                                act(PFt[:, i, :TT], ps[pi][:, :TT], AF.Copy, [PSN[pi]], ["APFt"])
                            else:
                                dve(lambda e: e.tensor_copy(out=PFt[:, i, :TT], in_=ps[pi][:, :TT]), [PSN[pi]], ["APFt"])
                        stg(PF[g].rearrange("(c p) t -> p c t", p=128)[:, :, t0:t0 + TT], PFt[:, :, :TT], ["APFt"])
                        NT = min(128, TT)
                        for j in range(TT // NT):
                            for gi, (woff, wn, poff) in enumerate(TGRP):
                                pi = nextps()
                                for c in range(8):
                                    mm(pi, ps[pi][:NT, :wn], hT[:, c, j * NT:(j + 1) * NT], Win[:, c, woff:woff + wn], c == 0, c == 7, ["Win", "AhT"])
                                if gi % 2 == 0:
                                    act(PTt[:NT, poff:poff + wn], ps[pi][:NT, :wn], AF.Copy, [PSN[pi]], ["APTt"])
                                else:
                                    dve(lambda e: e.tensor_copy(out=PTt[:NT, poff:poff + wn], in_=ps[pi][:NT, :wn]), [PSN[pi]], ["APTt"])
                            dve(lambda e: e.tensor_copy(out=VAt[:NT, :], in_=PTt[:NT, 256:512]), ["APTt"], ["AVAt"])
                            r0 = t0 + j * NT
                            stg(PT[g][r0:r0 + NT, :], PTt[:NT, :], ["APTt"])
                            stg(VA[g][PG[g] + r0:PG[g] + r0 + NT, :], VAt[:NT, :], ["AVAt"])
                            stg(o_sbk[g][l].rearrange("h t d -> t h d")[r0:r0 + NT], PTt[:NT, 0:256].rearrange("p (h d) -> p h d", d=64), ["APTt"])
                            stg(o_sbv[g][l].rearrange("h t d -> t h d")[r0:r0 + NT], PTt[:NT, 256:512].rearrange("p (h d) -> p h d", d=64), ["APTt"])
                S.barrier()

        def phase_attn(l, g):
            T = TG[g]; P0 = PG[g]; Tk = P0 + T
            NKB = (Tk + 127) // 128
            with contextlib.ExitStack() as ph:
                Kt = sb(ph, "atK", [128, 2, Tk], BF16)
                Vt = sb(ph, "atV", [128, NKB, 256], BF16)
                Qt = sb(ph, "atQ", [128, 2, 512], BF16)
                Qn = sb(ph, "atQn", [128, 2, 512], BF16)
                et = [sb(ph, "ate%d" % i, [128, 512]) for i in range(2)]
                spt = [sb(ph, "atsp%d" % i, [128, 512]) for i in range(2)]
                sps = [sb(ph, "atss%d" % i, [128, 512]) for i in range(2)]
                At = [sb(ph, "atA%d" % i, [128, 512], BF16) for i in range(2)]
                ost = [sb(ph, "ato%d" % i, [64, 512], BF16) for i in range(2)]
                if P0 > 0:
                    ckt = sb(ph, "atck", [128, 256])
                    for kb in range(P0 // 128):
                        for h4 in range(4):
                            ld(ckt[:, 64 * h4:64 * h4 + 64], cache_k[l, h4, kb * 128:(kb + 1) * 128, :], ["atck"])
                        pi = nextps()
                        for hp in range(2):
                            tr(pi, ps[pi][:, hp * 128:(hp + 1) * 128], ckt[:, hp * 128:(hp + 1) * 128], 128, ["atck"])
                        act(Kt[:, :, kb * 128:(kb + 1) * 128], ps[pi][:, 0:256].rearrange("p (c t) -> p c t", t=128), AF.Copy, [PSN[pi]], ["atK"])
                    for kb in range(P0 // 128):
                        for h4 in range(4):
                            ld(ckt[:, 64 * h4:64 * h4 + 64], cache_v[l, h4, kb * 128:(kb + 1) * 128, :], ["atck"])
                        dve(lambda e: e.tensor_copy(out=Vt[:, kb, :], in_=ckt[:, :]), ["atck"], ["atV"])
                    ld(Kt[:, :, P0:Tk], KAT[g].rearrange("(c p) t -> p c t", p=128)[:, :, P0:Tk], ["atK"])
                    ld(Vt[:T, P0 // 128, :], VA[g][P0:Tk, :], ["atV"])
                else:
                    ld(Kt[:, :, :], KAT[g].rearrange("(c p) t -> p c t", p=128), ["atK"])
                    ld(Vt[:, :, :], VA[g].rearrange("(b p) c -> p b c", p=128), ["atV"])
                if g == "s" and KSUB <= 1:
                    S.barrier()
                    return
                def cut(level, hh_=0):
                    if g == "s" and KSUB == level and hh_ == KHH:
                        raise _Cut()
                try:
                  for (q0, Nq) in tiles_of(T, 512):
                      ld(Qt[:, :, :Nq], QAT[g].rearrange("(c p) t -> p c t", p=128)[:, :, q0:q0 + Nq], ["atQ"])
                      dve(lambda e: e.tensor_scalar(out=Qn[:, :, :Nq], in0=Qt[:, :, :Nq], scalar1=-1.0, scalar2=None, op0=ALU.mult), ["atQ"], ["atQn"])
                      qa0 = P0 + q0
                      kbs = list(range((qa0 + Nq + 127) // 128 - 1, -1, -1))
                      for hp in range(2):
                          for n, kb in enumerate(kbs):
                              k0 = kb * 128
                              nk = min(128, Tk - k0)
                              diag = (k0 + nk > qa0)
                              jm = (k0 - qa0) // 128 if diag else 0
                              for hh in range(2):
                                  h = 2 * hp + hh
                                  po = 64 * hh
                                  zb, rb, ob = hh, 2 + hh, 4 + hh
                                  R = lambda s: s + str(hh)
                                  Kh = Kt[po:po + 64, hp, k0:k0 + nk]
                                  Qh = Qt[po:po + 64, hp, :Nq]
                                  mm(zb, ps[zb][:nk, :Nq], Kh, Qh, True, True, ["atK", "atQ"])
                                  act(et[hh][:nk, :Nq], ps[zb][:nk, :Nq], AF.Exp, [PSN[zb]], [R("ate")])
                                  act(spt[hh][:nk, :Nq], et[hh][:nk, :Nq], AF.Ln, [R("ate")], [R("atsp")], bias=1.0)
                                  if diag:
                                      dve(lambda e: e.tensor_tensor(out=spt[hh][:nk, :Nq], in0=spt[hh][:nk, :Nq], in1=amask[:nk, jm, :Nq], op=ALU.mult), [R("atsp"), "amask"], [R("atsp")])
                                  cut(2, hh)
                                  if nk < 128:
                                      pool(lambda e: e.memset(spt[hh][nk:128, :Nq], 0.0), [], [R("atsp")])
                                  mm(rb, ps[rb][:nk, :Nq], Uge[:, :nk], spt[hh][:, :Nq], True, False, [R("atsp"), "Uge"])
                                  if n > 0:
                                      mm(rb, ps[rb][:nk, :Nq], ones[:, :nk], sps[hh][:, :Nq], False, False, [R("atss"), "ones"])
                                  mm(rb, ps[rb][:nk, :Nq], Kh, Qn[po:po + 64, hp, :Nq], False, True, ["atK", "atQn"])
                                  act(At[hh][:nk, :Nq], ps[rb][:nk, :Nq], AF.Exp, [PSN[rb]], [R("atA")], scale=-1.0)
                                  if diag:
                                      dve(lambda e: e.tensor_tensor(out=At[hh][:nk, :Nq], in0=At[hh][:nk, :Nq], in1=amask[:nk, jm, :Nq], op=ALU.mult), [R("atA"), "amask"], [R("atA")])
                                  cut(3, hh)
                                  if n == 0:
                                      if nk < 128:
                                          pool(lambda e: e.memset(sps[hh][:, :Nq], 0.0), [], [R("atss")])
                                      pool(lambda e: e.tensor_copy(out=sps[hh][:nk, :Nq], in_=spt[hh][:nk, :Nq]), [R("atsp")], [R("atss")])
                                  elif n < len(kbs) - 1:
                                      pool(lambda e: e.tensor_tensor(out=sps[hh][:nk, :Nq], in0=sps[hh][:nk, :Nq], in1=spt[hh][:nk, :Nq], op=ALU.add), [R("atsp"), R("atss")], [R("atss")])
                                  cut(4, hh)
                                  mm(ob, ps[ob][:64, :Nq], Vt[:nk, kb, 64 * h:64 * h + 64], At[hh][:nk, :Nq], n == 0, n == len(kbs) - 1, ["atV", R("atA")])
                                  cut(5, hh)
                              cut(6)
                          cut(7)
                          for hh in range(2):
                              h = 2 * hp + hh
                              ob = 4 + hh
                              dve(lambda e: e.tensor_copy(out=ost[hh][:, :Nq], in_=ps[ob][:64, :Nq]), [PSN[ob]], ["ato%d" % hh])
                              stg(mixT[g][64 * h:64 * h + 64, q0:q0 + Nq], ost[hh][:, :Nq], ["ato%d" % hh])
                except _Cut:
                    pass
                S.barrier()

        BD2 = sb(st, "BD2", [128, 128])
        pool(lambda e: e.memset(BD2[:], 1.0), [], ["BD2"])
        pool(lambda e: e.memset(BD2[0:64, 64:128], 0.0), [], ["BD2"])
        pool(lambda e: e.memset(BD2[64:128, 0:64], 0.0), [], ["BD2"])
        NLT64 = sb(st, "NLT64", [64, 64]); NEGM64 = sb(st, "NEGM64", [64, 64])
        dve(lambda e: e.tensor_scalar(out=NLT64[:, :], in0=LT64[:, :], scalar1=-1.0, scalar2=None, op0=ALU.mult), ["LT64"], ["NLT64"])
        dve(lambda e: e.tensor_scalar(out=NEGM64[:, :], in0=LT64[:, :], scalar1=1e30, scalar2=-1e30, op0=ALU.mult, op1=ALU.add), ["LT64"], ["NEGM64"])
        CM = sb(st, "CM", [64, 12, 64])
        ld(CM[:, :, :], cmask_in.rearrange("m p f -> p m f"), ["CM"])
        I64 = ident[:64, :64]
        O64 = ones[:64, :64]

        def small_rstd(ss, n, tag):
            dve(lambda e: e.tensor_scalar(out=ss, in0=ss, scalar1=1.0 / n, scalar2=EPS, op0=ALU.mult, op1=ALU.add), [tag], [tag])
            act(ss, ss, AF.Sqrt, [tag], [tag])
            dve(lambda e: e.reciprocal(out=ss, in_=ss), [tag], [tag])

        def phase_gdn(l, g):
            T = TG[g]
            with contextlib.ExitStack() as ph:
                X = sb(ph, "gX", [128, 6, 3 + 512]); Y = sb(ph, "gY", [128, 6, 512])
                sq = sb(ph, "gsq", [128, 4, 512]); rs = sb(ph, "grs", [128, 4, 512])
                Sst = sb(ph, "gS", [128, 2, 64])
                ptc = sb(ph, "gptc", [64, 264]); ktv = sb(ph, "gktv", [64, 512])
                gt = sb(ph, "ggt", [64, 40]); gcl = sb(ph, "ggcl", [128, 8])
                dg = sb(ph, "gdg", [64, 64]); a1 = sb(ph, "ga1", [64, 64]); a2 = sb(ph, "ga2", [64, 64])
                gSL = sb(ph, "ggSL", [64, 64]); gUT = sb(ph, "ggUT", [64, 64])
                Pm = [sb(ph, "gP%d" % i, [64, 64]) for i in range(2)]
                PTm = [sb(ph, "gPT%d" % i, [64, 64]) for i in range(2)]
                Ym = [sb(ph, "gYm%d" % i, [64, 64]) for i in range(2)]
                Ru = sb(ph, "gRu", [64, 64]); Rw = sb(ph, "gRw", [64, 64]); usb = sb(ph, "gusb", [64, 64])
                Xm = [sb(ph, "gXm%d" % i, [64, 64]) for i in range(2)]
                Ab = sb(ph, "gAb", [64, 64]); Nb = sb(ph, "gNb", [64, 64]); Z1 = sb(ph, "gZ1", [64, 64]); Y1 = sb(ph, "gY1", [64, 64])
                wT = sb(ph, "gwT", [128, 64]); vnew = sb(ph, "gvn", [64, 64]); o1 = sb(ph, "go1", [64, 64])
                qkm = sb(ph, "gqkm", [64, 64]); kdec = sb(ph, "gkdec", [64, 64]); osq = sb(ph, "gosq", [64, 64])
                ss = sb(ph, "gss", [64, 4]); zs = sb(ph, "gzs", [64, 256]); omix = sb(ph, "gomix", [64, 256])
                mst = sb(ph, "gmst", [128, 2, 64], BF16)
                Sv = "(hp hh) k v -> (hh k) hp v"
                if g == "s":
                    ld(Sst[:, :, :], st_gs[l].rearrange(Sv, hh=2), ["gS"])
                    for j in range(3):
                        ld(X[:, :, j], st_gconv[l, j].rearrange("(b p) -> p b", p=128), ["gX"], allow_slow_non_contiguous=True)
                else:
                    dve(lambda e: e.memset(Sst[:], 0.0), [], ["gS"])
                    dve(lambda e: e.memset(X[:, :, 0:3], 0.0), [], ["gX"])
                tl = tiles_of(T, 512)
                for ti, (t0, TT) in enumerate(tl):
                    ld(X[:, :, 3:3 + TT], PF[g].rearrange("(c p) t -> p c t", p=128)[:, 0:6, t0:t0 + TT], ["gX"])
                    for b in range(6):
                        en = dve
                        en(lambda e: e.tensor_scalar(out=Y[:, b, :TT], in0=X[:, b, 0:TT], scalar1=wcg[:, b, 0:1], scalar2=None, op0=ALU.mult), ["gX", "par"], ["gY%d" % b])
                        for j in range(1, 4):
                            en(lambda e: e.scalar_tensor_tensor(out=Y[:, b, :TT], in0=X[:, b, j:j + TT], scalar=wcg[:, b, j:j + 1], in1=Y[:, b, :TT], op0=ALU.mult, op1=ALU.add), ["gX", "par", "gY%d" % b], ["gY%d" % b])
                    if ti == len(tl) - 1:
                        for j in range(3):
                            stg(o_gconv[g][l, j].rearrange("(b p) -> p b", p=128), X[:, :, TT + j], ["gX"], allow_slow_non_contiguous=True)
                    YR = ["gY%d" % b for b in range(6)]
                    dve(lambda e: e.tensor_copy(out=X[:, :, 0:3], in_=X[:, :, TT:TT + 3]), ["gX"] + YR, ["gX"])
                    act(Y[:, :, :TT], Y[:, :, :TT], AF.Silu, YR, YR)
                    act(sq[:, :, :TT], Y[:, 0:4, :TT], AF.Square, YR, ["gsq"])
                    for b in range(4):
                        pi = nextps()
                        mm(pi, ps[pi][:, :TT], BD2[:, :], sq[:, b, :TT], True, True, ["gsq", "BD2"])
                        dve(lambda e: e.tensor_scalar(out=rs[:, b, :TT], in0=ps[pi][:, :TT], scalar1=EPS, scalar2=None, op0=ALU.add), [PSN[pi]], ["grs"])
                    act(rs[:, :, :TT], rs[:, :, :TT], AF.Sqrt, ["grs"], ["grs"])
                    dve(lambda e: e.reciprocal(out=rs[:, :, :TT], in_=rs[:, :, :TT]), ["grs"], ["grs"])
                    for b in range(4):
                        dve(lambda e: e.scalar_tensor_tensor(out=Y[:, b, :TT], in0=Y[:, b, :TT], scalar=0.125 if b < 2 else 1.0, in1=rs[:, b, :TT], op0=ALU.mult, op1=ALU.mult), ["grs"] + YR, YR)
                    for c in range(TT // 64):
                        cs = slice(c * 64, c * 64 + 64)
                        r0 = t0 + c * 64
                        ld(ptc[:, :], PT[g][r0:r0 + 64, 512:776], ["gptc"])
                        pi = nextps()
                        for i, b in enumerate((2, 3, 4, 5)):
                            tr(pi, ps[pi][:64, i * 128:(i + 1) * 128], Y[:, b, cs], 128, YR)
                        act(ktv[:, :], ps[pi][:64, :], AF.Copy, [PSN[pi]], ["gktv"])
                        act(gt[:, 0:4], ptc[:, 256:260], AF.Sigmoid, ["gptc"], ["ggt"])
                        dve(lambda e: e.tensor_tensor(out=gt[:, 24:28], in0=ptc[:, 260:264], in1=dtb[:64, :], op=ALU.add), ["gptc", "par"], ["ggt"])
                        act(gt[:, 24:28], gt[:, 24:28], AF.Exp, ["ggt"], ["ggt"])
                        act(gt[:, 24:28], gt[:, 24:28], AF.Ln, ["ggt"], ["ggt"], bias=1.0)
                        dve(lambda e: e.tensor_tensor(out=gt[:, 4:8], in0=gt[:, 24:28], in1=nexpa[:64, :], op=ALU.mult), ["ggt", "par2"], ["ggt"])
                        pg = nextps()
                        mm(pg, ps[pg][:64, 0:4], UT64[:, :], gt[:, 4:8], True, True, ["ggt", "UT64"])
                        mm(pg, ps[pg][:, 4:8], ones[:64, :], gt[:, 4:8], True, True, ["ggt", "ones"])
                        dve(lambda e: e.tensor_copy(out=gcl[:, 4:8], in_=ps[pg][:, 4:8]), [PSN[pg]], ["ggcl"])
                        dve(lambda e: e.tensor_copy(out=gcl[:64, 0:4], in_=ps[pg][:64, 0:4]), [PSN[pg]], ["ggcl"])
                        act(gt[:, 8:12], gcl[:64, 0:4], AF.Exp, ["ggcl"], ["ggt"])
                        dve(lambda e: e.tensor_tensor(out=gt[:, 12:16], in0=gcl[:64, 4:8], in1=gcl[:64, 0:4], op=ALU.subtract), ["ggcl"], ["ggt"])
                        act(gt[:, 12:16], gt[:, 12:16], AF.Exp, ["ggt"], ["ggt"])
                        act(gcl[:, 4:8], gcl[:, 4:8], AF.Exp, ["ggcl"], ["ggcl"])
                        dve(lambda e: e.tensor_scalar(out=gt[:, 16:20], in0=gt[:, 0:4], scalar1=-1.0, scalar2=None, op0=ALU.mult), ["ggt"], ["ggt"])
                        dve(lambda e: e.tensor_tensor(out=gt[:, 20:24], in0=gt[:, 0:4], in1=gt[:, 8:12], op=ALU.mult), ["ggt"], ["ggt"])
                        act(zs[:, :], ptc[:, 0:256], AF.Silu, ["gptc"], ["gzs"])
                        for h in range(4):
                            hp, po = h // 2, 64 * (h % 2)
                            QnT = Y[po:po + 64, hp, cs]; KnT = Y[po:po + 64, 2 + hp, cs]
                            kn = ktv[:, 64 * h:64 * h + 64]; vv = ktv[:, 256 + 64 * h:256 + 64 * h + 64]
                            Sh = Sst[po:po + 64, hp, :]
                            dve(lambda e: e.tensor_scalar(out=dg[:, :], in0=I64, scalar1=gcl[:64, h:h + 1], scalar2=None, op0=ALU.mult), ["ident", "ggcl"], ["gdg"])
                            pD = nextps()
                            mm(pD, ps[pD][:64, :64], O64, dg[:, :], True, True, ["gdg", "ones"])
                            dve(lambda e: e.scalar_tensor_tensor(out=a1[:, :], in0=ps[pD][:64, :64], scalar=gcl[:64, h:h + 1], in1=LT64[:, :], op0=ALU.subtract, op1=ALU.mult), [PSN[pD], "ggcl", "LT64"], ["ga1"])
                            dve(lambda e: e.scalar_tensor_tensor(out=a2[:, :], in0=ps[pD][:64, :64], scalar=gcl[:64, h:h + 1], in1=UT64[:, :], op0=ALU.subtract, op1=ALU.mult), [PSN[pD], "ggcl", "UT64"], ["ga2"])
                            act(a1[:, :], a1[:, :], AF.Exp, ["ga1"], ["ga1"], scale=-1.0)
                            act(a2[:, :], a2[:, :], AF.Exp, ["ga2"], ["ga2"])
                            dve(lambda e: e.tensor_tensor(out=gSL[:, :], in0=a1[:, :], in1=SL64[:, :], op=ALU.mult), ["ga1", "SL64"], ["ggSL"])
                            dve(lambda e: e.tensor_tensor(out=gUT[:, :], in0=a2[:, :], in1=UT64[:, :], op=ALU.mult), ["ga2", "UT64"], ["ggUT"])
                            pK = nextps()
                            mm(pK, ps[pK][:64, :64], KnT, KnT, True, True, YR)
                            dve(lambda e: e.scalar_tensor_tensor(out=PTm[0][:, :], in0=ps[pK][:64, :64], scalar=gt[:, 16 + h:17 + h], in1=gSL[:, :], op0=ALU.mult, op1=ALU.mult), [PSN[pK], "ggt", "ggSL"], ["gPT0"])
                            pA = nextps()
                            tr(pA, ps[pA][:64, :64], PTm[0][:, :], 64, ["gPT0"])
                            act(Pm[0][:, :], ps[pA][:64, :64], AF.Copy, [PSN[pA]], ["gP0"])
                            A0 = Pm[0]; N0 = PTm[0]
                            pool(lambda e: e.tensor_tensor(out=Ym[0][:, :], in0=A0[:, :], in1=CM[:, 0, :], op=ALU.mult), ["gP0", "CM"], ["gYm0"])
                            pool(lambda e: e.tensor_tensor(out=Xm[0][:, :], in0=N0[:, :], in1=CM[:, 1, :], op=ALU.mult), ["gPT0", "CM"], ["gXm0"])
                            dve(lambda e: e.tensor_tensor(out=Ym[0][:, :], in0=Ym[0][:, :], in1=I64, op=ALU.add), ["gYm0", "ident"], ["gYm0"])
                            dve(lambda e: e.tensor_tensor(out=Xm[0][:, :], in0=Xm[0][:, :], in1=I64, op=ALU.add), ["gXm0", "ident"], ["gXm0"])
                            cur = 0
                            for lv in range(1, 6):
                                nx = 1 - cur
                                Wc, Xc = Ym[cur], Xm[cur]
                                Wn_, Xn_ = "gYm%d" % cur, "gXm%d" % cur
                                pool(lambda e: e.tensor_tensor(out=Ab[:, :], in0=A0[:, :], in1=CM[:, 2 * lv, :], op=ALU.mult), ["gP0", "CM"], ["gAb"])
                                pool(lambda e: e.tensor_tensor(out=Nb[:, :], in0=N0[:, :], in1=CM[:, 2 * lv + 1, :], op=ALU.mult), ["gPT0", "CM"], ["gNb"])
                                p1 = nextps()
                                mm(p1, ps[p1][:64, :64], Nb[:, :], Wc[:, :], True, True, ["gNb", Wn_])
                                act(Z1[:, :], ps[p1][:64, :64], AF.Copy, [PSN[p1]], ["gZ1"])
                                p2 = nextps()
                                mm(p2, ps[p2][:64, :64], Xc[:, :], Z1[:, :], True, True, [Xn_, "gZ1"])
                                dve(lambda e: e.tensor_tensor(out=Ym[nx][:, :], in0=ps[p2][:64, :64], in1=Wc[:, :], op=ALU.add), [PSN[p2], Wn_], ["gYm%d" % nx])
                                if lv < 5:
                                    p3 = nextps()
                                    mm(p3, ps[p3][:64, :64], Ab[:, :], Xc[:, :], True, True, ["gAb", Xn_])
                                    act(Y1[:, :], ps[p3][:64, :64], AF.Copy, [PSN[p3]], ["gY1"])
                                    p4 = nextps()
                                    mm(p4, ps[p4][:64, :64], Wc[:, :], Y1[:, :], True, True, [Wn_, "gY1"])
                                    dve(lambda e: e.tensor_tensor(out=Xm[nx][:, :], in0=ps[p4][:64, :64], in1=Xc[:, :], op=ALU.add), [PSN[p4], Xn_], ["gXm%d" % nx])
                                cur = nx
                            TTm = Ym[cur]; TTn = "gYm%d" % cur
                            dve(lambda e: e.tensor_scalar(out=Ru[:, :], in0=vv, scalar1=gt[:, h:h + 1], scalar2=None, op0=ALU.mult), ["gktv", "ggt"], ["gRu"])
                            dve(lambda e: e.tensor_scalar(out=Rw[:, :], in0=kn, scalar1=gt[:, 20 + h:21 + h], scalar2=None, op0=ALU.mult), ["gktv", "ggt"], ["gRw"])
                            pu = nextps(); pw = nextps()
                            mm(pu, ps[pu][:64, :64], TTm[:, :], Ru[:, :], True, True, [TTn, "gRu"])
                            mm(pw, ps[pw][:64, :64], Rw[:, :], TTm[:, :], True, True, [TTn, "gRw"])
                            act(usb[:, :], ps[pu][:64, :64], AF.Copy, [PSN[pu]], ["gusb"])
                            dve(lambda e: e.tensor_copy(out=wT[po:po + 64, :], in_=ps[pw][:64, :64]), [PSN[pw]], ["gwT"])
                            pS = nextps()
                            mm(pS, ps[pS][:64, :64], wT[po:po + 64, :], Sh, True, True, ["gwT", "gS"])
                            dve(lambda e: e.tensor_tensor(out=vnew[:, :], in0=usb[:, :], in1=ps[pS][:64, :64], op=ALU.subtract), ["gusb", PSN[pS]], ["gvn"])
                            pq = nextps()
                            mm(pq, ps[pq][:64, :64], QnT, Sh, True, True, YR + ["gS"])
                            dve(lambda e: e.tensor_scalar(out=o1[:, :], in0=ps[pq][:64, :64], scalar1=gt[:, 8 + h:9 + h], scalar2=None, op0=ALU.mult), [PSN[pq], "ggt"], ["go1"])
                            pk2 = nextps()
                            mm(pk2, ps[pk2][:64, :64], KnT, QnT, True, True, YR)
                            dve(lambda e: e.tensor_tensor(out=qkm[:, :], in0=ps[pk2][:64, :64], in1=gUT[:, :], op=ALU.mult), [PSN[pk2], "ggUT"], ["gqkm"])
                            po2 = nextps()
                            mm(po2, ps[po2][:64, :64], qkm[:, :], vnew[:, :], True, True, ["gqkm", "gvn"])
                            dve(lambda e: e.tensor_tensor(out=o1[:, :], in0=o1[:, :], in1=ps[po2][:64, :64], op=ALU.add), ["go1", PSN[po2]], ["go1"])
                            dve(lambda e: e.tensor_scalar(out=kdec[:, :], in0=kn, scalar1=gt[:, 12 + h:13 + h], scalar2=None, op0=ALU.mult), ["gktv", "ggt"], ["gkdec"])
                            pU = nextps()
                            mm(pU, ps[pU][:64, :64], kdec[:, :], vnew[:, :], True, True, ["gkdec", "gvn"])
                            dve(lambda e: e.scalar_tensor_tensor(out=Sh, in0=Sh, scalar=gcl[po:po + 64, 4 + h:5 + h], in1=ps[pU][:64, :64], op0=ALU.mult, op1=ALU.add), ["gS", "ggcl", PSN[pU]], ["gS"])
                            dve(lambda e: e.tensor_tensor(out=osq[:, :], in0=o1[:, :], in1=o1[:, :], op=ALU.mult), ["go1"], ["gosq"])
                            dve(lambda e: e.tensor_reduce(out=ss[:, h:h + 1], in_=osq[:, :], axis=AX.X, op=ALU.add), ["gosq"], ["gss"])
                            small_rstd(ss[:, h:h + 1], 64, "gss")
                            dve(lambda e: e.scalar_tensor_tensor(out=o1[:, :], in0=o1[:, :], scalar=ss[:, h:h + 1], in1=gng[:64, :], op0=ALU.mult, op1=ALU.mult), ["go1", "gss", "par"], ["go1"])
                            dve(lambda e: e.tensor_tensor(out=omix[:, 64 * h:64 * h + 64], in0=o1[:, :], in1=zs[:, 64 * h:64 * h + 64], op=ALU.mult), ["go1", "gzs"], ["gomix"])
                        pm = nextps()
                        for i in range(2):
                            tr(pm, ps[pm][:, i * 64:(i + 1) * 64], omix[:, i * 128:(i + 1) * 128], 64, ["gomix"])
                        act(mst[:, :, :], ps[pm][:, 0:128].rearrange("p (c t) -> p c t", t=64), AF.Copy, [PSN[pm]], ["gmst"])
                        stg(mixT[g].rearrange("(c p) t -> p c t", p=128)[:, 2:4, r0:r0 + 64], mst[:, :, :], ["gmst"])
                stg(o_gs[g][l].rearrange(Sv, hh=2), Sst[:, :, :], ["gS"])
                S.barrier()

        def phase_mlstm(l, g):
            T = TG[g]
            SC = 128 ** -0.5
            with contextlib.ExitStack() as ph:
                QK = sb(ph, "mQK", [128, 8, 512])
                Cn = sb(ph, "mCn", [128, 4, 129]); mb = sb(ph, "mmb", [128, 4])
                ptc = sb(ph, "mptc", [64, 1544]); v1 = sb(ph, "mv1", [64, 4, 129])
                gt = sb(ph, "mgt", [64, 48]); g128 = sb(ph, "mg128", [128, 16])
                dg = sb(ph, "mdg", [64, 64]); dm = sb(ph, "mdm", [64, 64]); wm = sb(ph, "mwm", [64, 64])
                qk = sb(ph, "mqk", [64, 64]); qkT = sb(ph, "mqkT", [64, 64])
                t1 = sb(ph, "mt1", [64, 129]); tot = sb(ph, "mtot", [64, 129]); kw = sb(ph, "mkw", [64, 128])
                hs = sb(ph, "mhs", [64, 128]); hsq = sb(ph, "mhsq", [64, 128]); sg = sb(ph, "msg", [64, 512])
                hmix = sb(ph, "mhmix", [64, 512]); mst = sb(ph, "mmst", [128, 4, 64], BF16)
                if g == "s":
                    ld(Cn[:, :, 0:128], st_mc[l].rearrange("h k v -> k h v"), ["mCn"])
                    ld(Cn[:, :, 128:129], st_mn[l].rearrange("h (k o) -> k h o", o=1), ["mCn"], allow_slow_non_contiguous=True)
                    ld(mb[:, :], st_mm[l].partition_broadcast(128), ["mmb"])
                else:
                    dve(lambda e: e.memset(Cn[:], 0.0), [], ["mCn"])
                    dve(lambda e: e.memset(mb[:], NEG), [], ["mmb"])
                dve(lambda e: e.memset(v1[:, :, 128:129], 1.0), [], ["mv1"])
                for (t0, TT) in tiles_of(T, 512):
                    ld(QK[:, :, :TT], PF[g].rearrange("(c p) t -> p c t", p=128)[:, 6:14, t0:t0 + TT], ["mQK"])
                    for c in range(TT // 64):
                        cs = slice(c * 64, c * 64 + 64)
                        r0 = t0 + c * 64
                        ld(ptc[:, :], PT[g][r0:r0 + 64, 776:2320], ["mptc"])
                        dve(lambda e: e.tensor_copy(out=v1[:, :, 0:128], in_=ptc[:, 520:1032].rearrange("p (h d) -> p h d", d=128)), ["mptc"], ["mv1"])
                        act(sg[:, :], ptc[:, 1032:1544], AF.Sigmoid, ["mptc"], ["msg"])
                        dve(lambda e: e.tensor_tensor(out=gt[:, 0:4], in0=ptc[:, 0:4], in1=bi_b[:64, :], op=ALU.add), ["mptc", "par"], ["mgt"])
                        dve(lambda e: e.tensor_tensor(out=gt[:, 4:8], in0=ptc[:, 4:8], in1=bf_b[:64, :], op=ALU.add), ["mptc", "par"], ["mgt"])
                        act(gt[:, 4:8], gt[:, 4:8], AF.Exp, ["mgt"], ["mgt"], scale=-1.0)
                        act(gt[:, 4:8], gt[:, 4:8], AF.Ln, ["mgt"], ["mgt"], bias=1.0)
                        dve(lambda e: e.tensor_scalar(out=gt[:, 4:8], in0=gt[:, 4:8], scalar1=-1.0, scalar2=None, op0=ALU.mult), ["mgt"], ["mgt"])
                        pg = nextps()
                        mm(pg, ps[pg][:64, 0:4], UT64[:, :], gt[:, 4:8], True, True, ["mgt", "UT64"])
                        mm(pg, ps[pg][:, 4:8], ones[:64, :], gt[:, 4:8], True, True, ["mgt", "ones"])
                        dve(lambda e: e.tensor_copy(out=gt[:, 8:12], in_=ps[pg][:64, 0:4]), [PSN[pg]], ["mgt"])
                        dve(lambda e: e.tensor_copy(out=g128[:, 0:4], in_=ps[pg][:, 4:8]), [PSN[pg]], ["mg128"])
                        dve(lambda e: e.tensor_tensor(out=gt[:, 12:16], in0=gt[:, 8:12], in1=gt[:, 0:4], op=ALU.subtract), ["mgt"], ["mgt"])
                        dve(lambda e: e.tensor_tensor(out=gt[:, 16:20], in0=gt[:, 8:12], in1=mb[:64, :], op=ALU.add), ["mgt", "mmb"], ["mgt"])
                        for h in range(4):
                            QT = QK[:, h, cs]; KT = QK[:, 4 + h, cs]
                            kk = ptc[:, 8 + 128 * h:8 + 128 * h + 128]
                            R = lambda s: s
                            dve(lambda e: e.tensor_scalar(out=dg[:, :], in0=I64, scalar1=gt[:, 12 + h:13 + h], scalar2=None, op0=ALU.mult), ["ident", "mgt"], ["mdg"])
                            pD = nextps()
                            mm(pD, ps[pD][:64, :64], O64, dg[:, :], True, True, ["mdg", "ones"])
                            dve(lambda e: e.scalar_tensor_tensor(out=dm[:, :], in0=ps[pD][:64, :64], scalar=gt[:, 8 + h:9 + h], in1=NLT64[:, :], op0=ALU.subtract, op1=ALU.mult), [PSN[pD], "mgt", "NLT64"], ["mdm"])
                            dve(lambda e: e.tensor_tensor(out=dm[:, :], in0=dm[:, :], in1=NEGM64[:, :], op=ALU.add), ["mdm", "NEGM64"], ["mdm"])
                            dve(lambda e: e.tensor_reduce(out=gt[:, 40 + h:41 + h], in_=dm[:, :], axis=AX.X, op=ALU.max), ["mdm"], ["mgt"])
                            dve(lambda e: e.tensor_tensor(out=gt[:, 20 + h:21 + h], in0=gt[:, 40 + h:41 + h], in1=gt[:, 16 + h:17 + h], op=ALU.max), ["mgt"], ["mgt"])
                            dve(lambda e: e.tensor_scalar(out=gt[:, 28 + h:29 + h], in0=gt[:, 20 + h:21 + h], scalar1=-1.0, scalar2=None, op0=ALU.mult), ["mgt"], ["mgt"])
                            act(wm[:, :], dm[:, :], AF.Exp, ["mdm", "mgt"], ["mwm"], bias=gt[:, 28 + h:29 + h])
                            act(gt[:, 24 + h:25 + h], gt[:, 16 + h:17 + h], AF.Exp, ["mgt"], ["mgt"], bias=gt[:, 28 + h:29 + h])
                            pq = nextps()
                            mm(pq, ps[pq][:64, :64], QT, KT, True, True, ["mQK"])
                            dve(lambda e: e.scalar_tensor_tensor(out=qk[:, :], in0=ps[pq][:64, :64], scalar=SC, in1=wm[:, :], op0=ALU.mult, op1=ALU.mult), [PSN[pq], "mwm"], ["mqk"])
                            pt = nextps()
                            tr(pt, ps[pt][:64, :64], qk[:, :], 64, ["mqk"])
                            act(qkT[:, :], ps[pt][:64, :64], AF.Copy, [PSN[pt]], ["mqkT"])
                            pc = nextps(); pv = nextps()
                            mm(pc, ps[pc][:64, :129], QT, Cn[:, h, :], True, True, ["mQK", "mCn"])
                            mm(pv, ps[pv][:64, :129], qkT[:, :], v1[:, h, :], True, True, ["mqkT", "mv1"])
                            dve(lambda e: e.tensor_scalar(out=t1[:, :], in0=ps[pc][:64, :129], scalar1=gt[:, 24 + h:25 + h], scalar2=None, op0=ALU.mult), [PSN[pc], "mgt"], ["mt1"])
                            dve(lambda e: e.tensor_tensor(out=tot[:, :], in0=t1[:, :], in1=ps[pv][:64, :129], op=ALU.add), ["mt1", PSN[pv]], ["mtot"])
                            dve(lambda e: e.tensor_scalar(out=gt[:, 36:37], in0=tot[:, 128:129], scalar1=-1.0, scalar2=None, op0=ALU.mult), ["mtot"], ["mgt"])
                            dve(lambda e: e.tensor_tensor(out=gt[:, 36:37], in0=gt[:, 36:37], in1=tot[:, 128:129], op=ALU.max), ["mtot", "mgt"], ["mgt"])
                            act(gt[:, 37:38], gt[:, 28 + h:29 + h], AF.Exp, ["mgt"], ["mgt"])
                            dve(lambda e: e.tensor_tensor(out=gt[:, 36:37], in0=gt[:, 36:37], in1=gt[:, 37:38], op=ALU.max), ["mgt"], ["mgt"])
                            dve(lambda e: e.reciprocal(out=gt[:, 36:37], in_=gt[:, 36:37]), ["mgt"], ["mgt"])
                            dve(lambda e: e.tensor_scalar(out=hs[:, :], in0=tot[:, 0:128], scalar1=gt[:, 36:37], scalar2=None, op0=ALU.mult), ["mtot", "mgt"], ["mhs"])
                            pm_ = nextps()
                            mm(pm_, ps[pm_][:, 0:1], SELL[:, :], gt[:, 20 + h:21 + h], True, True, ["mgt", "SELL"])
                            dve(lambda e: e.tensor_copy(out=g128[:, 4 + h:5 + h], in_=ps[pm_][:, 0:1]), [PSN[pm_]], ["mg128"])
                            dve(lambda e: e.tensor_tensor(out=g128[:, 8 + h:9 + h], in0=g128[:, h:h + 1], in1=mb[:, h:h + 1], op=ALU.add), ["mg128", "mmb"], ["mg128"])
                            dve(lambda e: e.tensor_tensor(out=g128[:, 8 + h:9 + h], in0=g128[:, 8 + h:9 + h], in1=g128[:, 4 + h:5 + h], op=ALU.subtract), ["mg128"], ["mg128"])
                            act(g128[:, 8 + h:9 + h], g128[:, 8 + h:9 + h], AF.Exp, ["mg128"], ["mg128"])
                            dve(lambda e: e.tensor_tensor(out=gt[:, 32 + h:33 + h], in0=g128[:64, h:h + 1], in1=gt[:, 12 + h:13 + h], op=ALU.subtract), ["mg128", "mgt"], ["mgt"])
                            dve(lambda e: e.tensor_tensor(out=gt[:, 32 + h:33 + h], in0=gt[:, 32 + h:33 + h], in1=g128[:64, 4 + h:5 + h], op=ALU.subtract), ["mg128", "mgt"], ["mgt"])
                            act(gt[:, 32 + h:33 + h], gt[:, 32 + h:33 + h], AF.Exp, ["mgt"], ["mgt"])
                            dve(lambda e: e.tensor_scalar(out=kw[:, :], in0=kk, scalar1=gt[:, 32 + h:33 + h], scalar2=SC, op0=ALU.mult, op1=ALU.mult), ["mptc", "mgt"], ["mkw"])
                            pU = nextps()
                            mm(pU, ps[pU][:, :129], kw[:, :], v1[:, h, :], True, True, ["mkw", "mv1"])
                            dve(lambda e: e.scalar_tensor_tensor(out=Cn[:, h, :], in0=Cn[:, h, :], scalar=g128[:, 8 + h:9 + h], in1=ps[pU][:, :129], op0=ALU.mult, op1=ALU.add), ["mCn", "mg128", PSN[pU]], ["mCn"])
                            dve(lambda e: e.tensor_copy(out=mb[:, h:h + 1], in_=g128[:, 4 + h:5 + h]), ["mg128"], ["mmb"])
                            dve(lambda e: e.tensor_tensor(out=hsq[:, :], in0=hs[:, :], in1=hs[:, :], op=ALU.mult), ["mhs"], ["mhsq"])
                            dve(lambda e: e.tensor_reduce(out=gt[:, 44 + h:45 + h], in_=hsq[:, :], axis=AX.X, op=ALU.add), ["mhsq"], ["mgt"])
                            small_rstd(gt[:, 44 + h:45 + h], 128, "mgt")
                            dve(lambda e: e.scalar_tensor_tensor(out=hs[:, :], in0=hs[:, :], scalar=gt[:, 44 + h:45 + h], in1=mng[:64, :], op0=ALU.mult, op1=ALU.mult), ["mhs", "mgt", "par"], ["mhs"])
                            dve(lambda e: e.tensor_tensor(out=hmix[:, 128 * h:128 * h + 128], in0=hs[:, :], in1=sg[:, 128 * h:128 * h + 128], op=ALU.mult), ["mhs", "msg"], ["mhmix"])
                        pm = nextps()
                        for i in range(4):
                            tr(pm, ps[pm][:, i * 64:(i + 1) * 64], hmix[:, i * 128:(i + 1) * 128], 64, ["mhmix"])
                        act(mst[:, :, :], ps[pm][:, 0:256].rearrange("p (c t) -> p c t", t=64), AF.Copy, [PSN[pm]], ["mmst"])
                        stg(mixT[g].rearrange("(c p) t -> p c t", p=128)[:, 4:8, r0:r0 + 64], mst[:, :, :], ["mmst"])
                stg(o_mc[g][l].rearrange("h k v -> k h v"), Cn[:, :, 0:128], ["mCn"])
                stg(o_mn[g][l].rearrange("h (k o) -> k h o", o=1), Cn[:, :, 128:129], ["mCn"], allow_slow_non_contiguous=True)
                stg(o_mm[g][l].rearrange("(o h) -> o h", o=1), mb[0:1, :], ["mmb"])
                S.barrier()

        def phase_C1(l):
            with contextlib.ExitStack() as ph:
                Wo = sb(ph, "Wo", [128, 8, D], BF16)
                for c in range(8):
                    ld(Wo[:, c, :], w_out[l, c * 128:(c + 1) * 128, :], ["Wo"], q="pool")
                mt = sb(ph, "c1m", [128, 8, 512], BF16); yT = sb(ph, "c1y", [128, 8, 512])
                sq = sb(ph, "c1sq", [128, 8, 512]); rstd = sb(ph, "c1r", [128, 512]); xt = sb(ph, "c1x", [128, 8, 512])
                for g in "ps":
                    for (t0, TT) in tiles_of(TG[g], 512):
                        ld(mt[:, :, :TT], mixT[g].rearrange("(c p) t -> p c t", p=128)[:, :, t0:t0 + TT], ["c1m"])
                        ld(xt[:, :, :TT], xT[g].rearrange("(c p) t -> p c t", p=128)[:, :, t0:t0 + TT], ["c1x"])
                        for ob in range(8):
                            pi = nextps()
                            for c in range(8):
                                mm(pi, ps[pi][:, :TT], Wo[:, c, ob * 128:(ob + 1) * 128], mt[:, c, :TT], c == 0, c == 7, ["Wo", "c1m"])
                            act(yT[:, ob, :TT], ps[pi][:, :TT], AF.Copy, [PSN[pi]], ["c1y"])
                        fm_rstd(yT, sq, rstd, TT, "c1y", "c1sq", "c1r")
                        for ob in range(8):
                            dve(lambda e: e.scalar_tensor_tensor(out=yT[:, ob, :TT], in0=yT[:, ob, :TT], scalar=gpost[:, ob:ob + 1], in1=rstd[:, :TT], op0=ALU.mult, op1=ALU.mult), ["c1y", "c1r", "par"], ["c1y"])
                        pool(lambda e: e.tensor_tensor(out=xt[:, :, :TT], in0=xt[:, :, :TT], in1=yT[:, :, :TT], op=ALU.add), ["c1x", "c1y"], ["c1x"])
                        stg(x1T[g].rearrange("(c p) t -> p c t", p=128)[:, :, t0:t0 + TT], xt[:, :, :TT], ["c1x"])
                S.barrier()

        def phase_C2(l, hf, last):
            HB = 11
            c0 = hf * HB * 128
            with contextlib.ExitStack() as ph:
                Wg = sb(ph, "Wg", [128, 8, HB * 128], BF16); Wu = sb(ph, "Wu", [128, 8, HB * 128], BF16)
                Wd = sb(ph, "Wd", [128, HB, D], BF16)
                for c in range(8):
                    ld(Wg[:, c, :], ffn_w_up[l, c * 128:(c + 1) * 128, c0:c0 + HB * 128], ["Wg"], q="pool")
                    ld(Wu[:, c, :], ffn_w_up[l, c * 128:(c + 1) * 128, DFF + c0:DFF + c0 + HB * 128], ["Wu"], q="pool")
                for fb in range(HB):
                    ld(Wd[:, fb, :], ffn_w_down[l, c0 + fb * 128:c0 + (fb + 1) * 128, :], ["Wd"], q="pool")
                x1 = sb(ph, "c2x", [128, 8, 512]); sq = sb(ph, "c2sq", [128, 8, 512]); rstd = sb(ph, "c2r", [128, 512])
                hT = sb(ph, "c2h", [128, 8, 512], BF16)
                G = [sb(ph, "c2G%d" % i, [128, 2 + 512]) for i in range(2)]
                halo = sb(ph, "c2halo", [128, HB, 2])
                cv = [sb(ph, "c2cv%d" % i, [128, 512]) for i in range(2)]
                aT = sb(ph, "c2a", [128, HB, 512], BF16)
                y2 = sb(ph, "c2y", [128, 8, 512])
                yo = sb(ph, "c2yo", [128, D]) if (last and hf == 1) else None
                for g in "ps":
                    T = TG[g]
                    if g == "s":
                        for j in range(2):
                            ld(halo[:, :, j], st_fconv[l, j].rearrange("(b p) -> p b", p=128)[:, HB * hf:HB * hf + HB], ["c2halo"], allow_slow_non_contiguous=True)
                    else:
                        dve(lambda e: e.memset(halo[:], 0.0), [], ["c2halo"])
                    tl = tiles_of(T, 512)
                    for ti, (t0, TT) in enumerate(tl):
                        ld(x1[:, :, :TT], x1T[g].rearrange("(c p) t -> p c t", p=128)[:, :, t0:t0 + TT], ["c2x"])
                        fm_rstd(x1, sq, rstd, TT, "c2x", "c2sq", "c2r")
                        for c in range(8):
                            dve(lambda e: e.scalar_tensor_tensor(out=hT[:, c, :TT], in0=x1[:, c, :TT], scalar=gfpre[:, c:c + 1], in1=rstd[:, :TT], op0=ALU.mult, op1=ALU.mult), ["c2x", "c2r", "par"], ["c2h"])
                        if hf == 1:
                            ld(sq[:, :, :TT], y2p[g].rearrange("(c p) t -> p c t", p=128)[:, :, t0:t0 + TT], ["c2sq"])
                        for fb in range(HB):
                            b2 = fb % 2
                            Gb = G[b2]; Gn = "c2G%d" % b2; cvb = cv[b2]; cn = "c2cv%d" % b2
                            fbg = HB * hf + fb
                            pg = nextps(); pu = nextps()
                            for c in range(8):
                                mm(pg, ps[pg][:, :TT], Wg[:, c, fb * 128:(fb + 1) * 128], hT[:, c, :TT], c == 0, c == 7, ["Wg", "c2h"])
                            for c in range(8):
                                mm(pu, ps[pu][:, :TT], Wu[:, c, fb * 128:(fb + 1) * 128], hT[:, c, :TT], c == 0, c == 7, ["Wu", "c2h"])
                            act(Gb[:, 2:2 + TT], ps[pg][:, :TT], AF.Copy, [PSN[pg]], [Gn])
                            pool(lambda e: e.tensor_copy(out=Gb[:, 0:2], in_=halo[:, fb, :]), ["c2halo"], [Gn])
                            pool(lambda e: e.tensor_copy(out=halo[:, fb, :], in_=Gb[:, TT:TT + 2]), [Gn], ["c2halo"])
                            dve(lambda e: e.tensor_scalar(out=cvb[:, :TT], in0=Gb[:, 0:TT], scalar1=wcf[:, fbg, 0:1], scalar2=None, op0=ALU.mult), [Gn, "par"], [cn])
                            for j in (1, 2):
                                dve(lambda e: e.scalar_tensor_tensor(out=cvb[:, :TT], in0=Gb[:, j:j + TT], scalar=wcf[:, fbg, j:j + 1], in1=cvb[:, :TT], op0=ALU.mult, op1=ALU.add), [Gn, "par", cn], [cn])
                            act(cvb[:, :TT], cvb[:, :TT], AF.Gelu_apprx_tanh, [cn], [cn])
                            dve(lambda e: e.tensor_tensor(out=aT[:, fb, :TT], in0=cvb[:, :TT], in1=ps[pu][:, :TT], op=ALU.mult), [cn, PSN[pu]], ["c2a"])
                        if ti == len(tl) - 1:
                            for j in range(2):
                                stg(o_fconv[g][l, j].rearrange("(b p) -> p b", p=128)[:, HB * hf:HB * hf + HB], halo[:, :, j], ["c2halo"], allow_slow_non_contiguous=True)
                        for ob in range(8):
                            pi = nextps()
                            for fb in range(HB):
                                mm(pi, ps[pi][:, :TT], Wd[:, fb, ob * 128:(ob + 1) * 128], aT[:, fb, :TT], fb == 0, fb == HB - 1, ["Wd", "c2a"])
                            if hf == 0:
                                act(y2[:, ob, :TT], ps[pi][:, :TT], AF.Copy, [PSN[pi]], ["c2y"])
                            else:
                                dve(lambda e: e.tensor_tensor(out=y2[:, ob, :TT], in0=sq[:, ob, :TT], in1=ps[pi][:, :TT], op=ALU.add), ["c2sq", PSN[pi]], ["c2y"])
                        if hf == 0:
                            stg(y2p[g].rearrange("(c p) t -> p c t", p=128)[:, :, t0:t0 + TT], y2[:, :, :TT], ["c2y"])
                            continue
                        fm_rstd(y2, sq, rstd, TT, "c2y", "c2sq", "c2r")
                        for ob in range(8):
                            dve(lambda e: e.scalar_tensor_tensor(out=y2[:, ob, :TT], in0=y2[:, ob, :TT], scalar=gfpost[:, ob:ob + 1], in1=rstd[:, :TT], op0=ALU.mult, op1=ALU.mult), ["c2y", "c2r", "par"], ["c2y"])
                        pool(lambda e: e.tensor_tensor(out=x1[:, :, :TT], in0=x1[:, :, :TT], in1=y2[:, :, :TT], op=ALU.add), ["c2x", "c2y"], ["c2x"])
                        if not last:
                            stg(xT[g].rearrange("(c p) t -> p c t", p=128)[:, :, t0:t0 + TT], x1[:, :, :TT], ["c2x"])
                        else:
                            NT = min(128, TT)
                            for j in range(TT // NT):
                                pa = nextps(); pb = nextps()
                                for c in range(8):
                                    pi = pa if c < 4 else pb
                                    tr(pi, ps[pi][:NT, (c % 4) * 128:(c % 4 + 1) * 128], x1[:, c, j * NT:(j + 1) * NT], 128, ["c2x"])
                                act(yo[:NT, 0:512], ps[pa][:NT, :], AF.Copy, [PSN[pa]], ["c2yo"])
                                dve(lambda e: e.tensor_copy(out=yo[:NT, 512:1024], in_=ps[pb][:NT, :]), [PSN[pb]], ["c2yo"])
                                stg(y_out[g][t0 + j * NT:t0 + (j + 1) * NT, :], yo[:NT, :], ["c2yo"])
                S.barrier()

        steps = [phase_T0]
        for l in range(2):
            steps.append(lambda l=l: load_params(l))
            steps.append(lambda l=l: phase_A(l))
            for g in "ps":
                steps.append(lambda l=l, g=g: phase_attn(l, g))
                steps.append(lambda l=l, g=g: phase_gdn(l, g))
                steps.append(lambda l=l, g=g: phase_mlstm(l, g))
            steps.append(lambda l=l: phase_C1(l))
            steps.append(lambda l=l: phase_C2(l, 0, l == 1))
            steps.append(lambda l=l: phase_C2(l, 1, l == 1))
        for i, stp in enumerate(steps):
            if i < KLIM:
                stp()
        S.barrier()
        n_inst = S.n_inst
    return nc, n_inst


_CACHE = {}


def _level_masks():
    m = np.zeros((12, 64, 64), np.float32)
    i = np.arange(64)
    for lv in range(6):
        b = 1 << lv
        same = (i[:, None] // (2 * b)) == (i[None, :] // (2 * b))
        first = (i % (2 * b)) < b
        mu = same & first[:, None] & (~first)[None, :]
        m[2 * lv] = mu
        m[2 * lv + 1] = mu.T
    return m


def _run(inputs, SEQ, PAST):
    key = (SEQ, PAST)
    if key not in _CACHE:
        _CACHE[key] = build(SEQ, PAST)
    nc, _ = _CACHE[key]
    f = lambda a: np.ascontiguousarray(np.asarray(a, dtype=np.float32))
    shared = ["g_mix_pre", "g_mix_post", "g_ffn_pre", "g_ffn_post", "w_in", "gdn_conv_w", "gdn_a_log", "gdn_dt_bias",
              "gdn_norm_g", "mlstm_b_i", "mlstm_b_f", "mlstm_norm_g", "w_out", "ffn_w_up", "ffn_conv_w", "ffn_w_down"]
    percore = ["cache_sb_k", "cache_sb_v", "state_gdn_conv", "state_gdn_s", "state_mlstm_c", "state_mlstm_n",
               "state_mlstm_m", "state_ffn_conv"]
    base = {k: f(inputs[k]) for k in shared}
    base["cmask"] = _level_masks()
    base["x_prompt"] = f(inputs["x_prompt"])[0]
    in_maps = []
    for c in range(8):
        m = dict(base)
        m["x_sample"] = f(inputs["x_sample"])[c]
        for k in percore:
            m[k] = f(np.asarray(inputs[k])[:, c])
        in_maps.append(m)
    res = run_bass_kernel_spmd(nc, in_maps, core_ids=list(range(8)))
    R = res.results
    outs = [R[0]["y_prompt"][None], np.stack([R[c]["y_sample"] for c in range(8)])]
    for nm in ("sb_k", "sb_v", "gdn_conv", "gdn_s", "mlstm_c", "mlstm_n", "mlstm_m", "ffn_conv"):
        outs.append(np.asarray(R[0][nm + "_p"])[:, None])
    for nm in ("sb_k", "sb_v", "gdn_conv", "gdn_s", "mlstm_c", "mlstm_n", "mlstm_m", "ffn_conv"):
        outs.append(np.stack([np.asarray(R[c][nm + "_s"]) for c in range(8)], axis=1))
    return tuple(np.ascontiguousarray(o, dtype=np.float32) for o in outs)


def kernel(**inputs):
    SEQ = int(np.asarray(inputs["x_prompt"]).shape[1])
    PAST = int(np.asarray(inputs["cache_sb_k"]).shape[3])
    return _run(inputs, SEQ, PAST)
```

```python
import contextlib
import numpy as np
import concourse.bass as bass
import concourse.mybir as mybir
from concourse.bass_utils import run_bass_kernel_spmd

F32 = mybir.dt.float32
BF16 = mybir.dt.bfloat16
AF = mybir.ActivationFunctionType
ALU = mybir.AluOpType
AX = mybir.AxisListType

N_DMA_SEMS = 24
KLIM = 99
SEM_EPOCH = 20000
KSUB = 99
KHH = 0
D = 1024
DFF = 2816
NCOL = 3856
EPS = 1e-6
NEG = -1e30


class _Cut(Exception):
    pass


class Sched:
    def __init__(self, nc, stack):
        self.nc = nc
        self.engs = {"pe": nc.tensor, "act": nc.scalar, "dve": nc.vector, "pool": nc.gpsimd, "sp": nc.sync}
        self.sems = {}
        self.cnt = {}
        for e in ("pe", "act", "dve", "pool"):
            self.sems[e] = stack.enter_context(nc.semaphore("s_" + e))
            self.cnt[e] = 0
        for i in range(N_DMA_SEMS):
            self.sems[("d", i)] = stack.enter_context(nc.semaphore("s_dma%d" % i))
            self.cnt[("d", i)] = 0
        self.dnext = 0
        self.stack = stack
        self.epoch = 0
        self.waited = {e: {} for e in self.engs}
        self.lastw = {}
        self.reads = {}
        self.n_inst = 0

    def _wait(self, e, evs):
        best = {}
        for ev in evs:
            if ev is None:
                continue
            k, v = ev
            if k == "pe" and e == "pe":
                continue
            if self.waited[e].get(k, 0) < v and best.get(k, 0) < v:
                best[k] = v
        for k, v in best.items():
            self.engs[e].wait_ge(self.sems[k], v)
            self.waited[e][k] = v

    def _deps(self, reads, writes):
        evs = []
        for r in reads:
            evs.append(self.lastw.get(r))
        for w in writes:
            evs.append(self.lastw.get(w))
            evs.extend(self.reads.get(w, ()))
        return evs

    def _commit(self, ev, reads, writes):
        for r in reads:
            lst = self.reads.setdefault(r, [])
            lst.append(ev)
            if len(lst) > 16:
                d = {}
                for k, v in lst:
                    if d.get(k, 0) < v:
                        d[k] = v
                self.reads[r] = list(d.items())
        for w in writes:
            self.lastw[w] = ev
            self.reads[w] = []

    def maybe_epoch(self):
        if max(self.cnt[e] for e in ("pe", "act", "dve", "pool")) < SEM_EPOCH:
            return
        evs = [(e, self.cnt[e]) for e in ("pe", "act", "dve", "pool") if self.cnt[e] > 0]
        for e in self.engs:
            self._wait(e, evs)
        self.epoch += 1
        for e in ("pe", "act", "dve", "pool"):
            self.sems[e] = self.stack.enter_context(self.nc.semaphore("s_%s_%d" % (e, self.epoch)))
            self.cnt[e] = 0
        comp = ("pe", "act", "dve", "pool")
        for e in self.engs:
            self.waited[e] = {k: v for k, v in self.waited[e].items() if k not in comp}
        self.lastw = {r: ev for r, ev in self.lastw.items() if ev[0] not in comp}
        self.reads = {r: [ev for ev in lst if ev[0] not in comp] for r, lst in self.reads.items()}

    def op(self, e, fn, reads=(), writes=()):
        self.maybe_epoch()
        self._wait(e, self._deps(reads, writes))
        ins = fn(self.engs[e])
        self.cnt[e] += 1
        ins.then_inc(self.sems[e], 1)
        ev = (e, self.cnt[e])
        self._commit(ev, reads, writes)
        self.n_inst += 1
        return ev

    def dma(self, q, out, in_, reads=(), writes=(), **kw):
        i = self.dnext
        self.dnext = (self.dnext + 1) % N_DMA_SEMS
        k = ("d", i)
        evs = self._deps(reads, writes)
        if self.cnt[k] > 0:
            evs.append((k, self.cnt[k]))
        self._wait(q, evs)
        ins = self.engs[q].dma_start(out=out, in_=in_, **kw)
        self.cnt[k] += 16
        ins.then_inc(self.sems[k], 16)
        ev = (k, self.cnt[k])
        self._commit(ev, reads, writes)
        self.n_inst += 1
        return ev

    def barrier(self):
        evs = [(k, c) for k, c in self.cnt.items() if c > 0]
        for e in self.engs:
            self._wait(e, evs)


def build(SEQ, PAST, NS=64):
    nc = bass.Bass("TRN2", target_bir_lowering=False)

    def din(name, shape):
        return nc.dram_tensor(name, list(shape), F32, kind="ExternalInput").ap()

    def dout(name, shape):
        return nc.dram_tensor(name, list(shape), F32, kind="ExternalOutput").ap()

    def dscr(name, shape, dt=F32):
        return nc.dram_tensor(name, list(shape), dt).ap()

    x_in = {"p": din("x_prompt", [SEQ, D]), "s": din("x_sample", [NS, D])}
    cache_k = din("cache_sb_k", [2, 4, PAST, 64])
    cache_v = din("cache_sb_v", [2, 4, PAST, 64])
    st_gconv = din("state_gdn_conv", [2, 3, 768])
    st_gs = din("state_gdn_s", [2, 4, 64, 64])
    st_mc = din("state_mlstm_c", [2, 4, 128, 128])
    st_mn = din("state_mlstm_n", [2, 4, 128])
    st_mm = din("state_mlstm_m", [2, 4])
    st_fconv = din("state_ffn_conv", [2, 2, DFF])
    g_mix_pre = din("g_mix_pre", [2, D]); g_mix_post = din("g_mix_post", [2, D])
    g_ffn_pre = din("g_ffn_pre", [2, D]); g_ffn_post = din("g_ffn_post", [2, D])
    w_in = din("w_in", [2, D, NCOL])
    gdn_conv_w = din("gdn_conv_w", [2, 4, 768])
    gdn_a_log = din("gdn_a_log", [2, 4]); gdn_dt_bias = din("gdn_dt_bias", [2, 4])
    gdn_norm_g = din("gdn_norm_g", [2, 64])
    mlstm_b_i = din("mlstm_b_i", [2, 4]); mlstm_b_f = din("mlstm_b_f", [2, 4])
    mlstm_norm_g = din("mlstm_norm_g", [2, 128])
    w_out = din("w_out", [2, D, D])
    ffn_w_up = din("ffn_w_up", [2, D, 2 * DFF])
    ffn_conv_w = din("ffn_conv_w", [2, 3, DFF])
    ffn_w_down = din("ffn_w_down", [2, DFF, D])

    cmask_in = din("cmask", [12, 64, 64])
    TG = {"p": SEQ, "s": NS}
    PG = {"p": 0, "s": PAST}
    y_out = {"p": dout("y_prompt", [SEQ, D]), "s": dout("y_sample", [NS, D])}
    o_sbk = {g: dout("sb_k_" + g, [2, 4, TG[g], 64]) for g in "ps"}
    o_sbv = {g: dout("sb_v_" + g, [2, 4, TG[g], 64]) for g in "ps"}
    o_gconv = {g: dout("gdn_conv_" + g, [2, 3, 768]) for g in "ps"}
    o_gs = {g: dout("gdn_s_" + g, [2, 4, 64, 64]) for g in "ps"}
    o_mc = {g: dout("mlstm_c_" + g, [2, 4, 128, 128]) for g in "ps"}
    o_mn = {g: dout("mlstm_n_" + g, [2, 4, 128]) for g in "ps"}
    o_mm = {g: dout("mlstm_m_" + g, [2, 4]) for g in "ps"}
    o_fconv = {g: dout("ffn_conv_" + g, [2, 2, DFF]) for g in "ps"}

    xT = {g: dscr("xT_" + g, [D, TG[g]]) for g in "ps"}
    x1T = {g: dscr("x1T_" + g, [D, TG[g]]) for g in "ps"}
    y2p = {g: dscr("y2p_" + g, [D, TG[g]]) for g in "ps"}
    PF = {g: dscr("PF_" + g, [1792, TG[g]]) for g in "ps"}
    PT = {g: dscr("PT_" + g, [TG[g], 2320]) for g in "ps"}
    QAT = {g: dscr("QAT_" + g, [256, TG[g]], BF16) for g in "ps"}
    KAT = {g: dscr("KAT_" + g, [256, PG[g] + TG[g]], BF16) for g in "ps"}
    VA = {g: dscr("VA_" + g, [PG[g] + TG[g], 256], BF16) for g in "ps"}
    mixT = {g: dscr("mixT_" + g, [D, TG[g]], BF16) for g in "ps"}

    with contextlib.ExitStack() as st:
        S = Sched(nc, st)

        uniq = [0]

        def sb(stack, name, shape, dt=F32):
            uniq[0] += 1
            return stack.enter_context(nc.sbuf_tensor("%s_%d" % (name, uniq[0]), list(shape), dt))

        ps = [st.enter_context(nc.psum_tensor("ps%d" % i, [128, 512], F32)) for i in range(8)]
        PSN = ["ps%d" % i for i in range(8)]
        rot = [0]

        def nextps(lo=0, hi=8):
            i = lo + rot[0] % (hi - lo)
            rot[0] += 1
            return i

        def mm(pi, out, lhsT, rhs, start, stop, reads):
            S.op("pe", lambda e: e.matmul(out, lhsT=lhsT, rhs=rhs, start=start, stop=stop), reads=reads, writes=[PSN[pi]])

        def tr(pi, out, in_, n, reads):
            S.op("pe", lambda e: e.transpose(out=out, in_=in_, identity=ident[:n, :n]), reads=list(reads) + ["ident"], writes=[PSN[pi]])

        def act(out, in_, func, reads, writes, bias=0.0, scale=1.0):
            S.op("act", lambda e: e.activation(out=out, in_=in_, func=func, bias=bias, scale=scale), reads=reads, writes=writes)

        def dve(fn, reads, writes):
            S.op("dve", fn, reads=reads, writes=writes)

        def pool(fn, reads, writes):
            S.op("pool", fn, reads=reads, writes=writes)

        def ld(out, in_, writes, q="sp", reads=(), **kw):
            S.dma(q, out, in_, reads=reads, writes=writes, **kw)

        def stg(out, in_, reads, q="sp", writes=(), **kw):
            S.dma(q, out, in_, reads=reads, writes=writes, **kw)

        ident = sb(st, "ident", [128, 128]); ones = sb(st, "ones", [128, 128])
        Uge = sb(st, "Uge", [128, 128])
        LT64 = sb(st, "LT64", [64, 64]); SL64 = sb(st, "SL64", [64, 64])
        UT64 = sb(st, "UT64", [64, 64]); SU64 = sb(st, "SU64", [64, 64])
        SELL = sb(st, "SELL", [64, 128])
        amask = sb(st, "amask", [128, 4, 512])
        pool(lambda e: e.memset(ones[:], 1.0), [], ["ones"])
        pool(lambda e: e.memset(amask[:], 1.0), [], ["amask"])

        def sel(out, in_, pattern, op, base, cm, reads, writes, fill=0.0):
            pool(lambda e: e.affine_select(out=out, in_=in_, pattern=pattern, compare_op=op, fill=fill, base=base, channel_multiplier=cm), reads, writes)

        sel(ident[:], ones[:], [[-1, 128]], ALU.is_equal, 0, 1, ["ones"], ["ident"])
        sel(Uge[:], ones[:], [[-1, 128]], ALU.is_ge, 0, 1, ["ones"], ["Uge"])
        sel(LT64[:], ones[:64, :64], [[-1, 64]], ALU.is_ge, 0, 1, ["ones"], ["LT64"])
        sel(SL64[:], ones[:64, :64], [[-1, 64]], ALU.is_ge, -1, 1, ["ones"], ["SL64"])
        sel(UT64[:], ones[:64, :64], [[1, 64]], ALU.is_ge, 0, -1, ["ones"], ["UT64"])
        sel(SU64[:], ones[:64, :64], [[1, 64]], ALU.is_ge, -1, -1, ["ones"], ["SU64"])
        sel(SELL[:], ones[:64, :], [[0, 128]], ALU.is_equal, -63, 1, ["ones"], ["SELL"])
        for j in range(4):
            sel(amask[:, j, :], amask[:, j, :], [[1, 512]], ALU.is_ge, -128 * j - 1, -1, ["amask"], ["amask"])

        gpre = sb(st, "gpre", [128, 8]); gpost = sb(st, "gpost", [128, 8])
        gfpre = sb(st, "gfpre", [128, 8]); gfpost = sb(st, "gfpost", [128, 8])
        wcg = sb(st, "wcg", [128, 6, 4]); wcf = sb(st, "wcf", [128, 22, 3])
        alog = sb(st, "alog", [128, 4]); dtb = sb(st, "dtb", [128, 4])
        bi_b = sb(st, "bi_b", [128, 4]); bf_b = sb(st, "bf_b", [128, 4])
        gng = sb(st, "gng", [128, 64]); mng = sb(st, "mng", [128, 128])
        nexpa = sb(st, "nexpa", [128, 4])

        def load_params(l):
            for t, src in ((gpre, g_mix_pre), (gpost, g_mix_post), (gfpre, g_ffn_pre), (gfpost, g_ffn_post)):
                ld(t[:], src[l].rearrange("(c p) -> p c", p=128), ["par"], allow_slow_non_contiguous=True)
            for j in range(4):
                ld(wcg[:, :, j], gdn_conv_w[l, j].rearrange("(b p) -> p b", p=128), ["par"], allow_slow_non_contiguous=True)
            for j in range(3):
                ld(wcf[:, :, j], ffn_conv_w[l, j].rearrange("(b p) -> p b", p=128), ["par"], allow_slow_non_contiguous=True)
            ld(alog[:], gdn_a_log[l].partition_broadcast(128), ["par"])
            ld(dtb[:], gdn_dt_bias[l].partition_broadcast(128), ["par"])
            ld(bi_b[:], mlstm_b_i[l].partition_broadcast(128), ["par"])
            ld(bf_b[:], mlstm_b_f[l].partition_broadcast(128), ["par"])
            ld(gng[:], gdn_norm_g[l].partition_broadcast(128), ["par"])
            ld(mng[:], mlstm_norm_g[l].partition_broadcast(128), ["par"])
            act(nexpa[:], alog[:], AF.Exp, ["par"], ["par2"])
            dve(lambda e: e.tensor_scalar(out=nexpa[:], in0=nexpa[:], scalar1=-1.0, scalar2=None, op0=ALU.mult), ["par2"], ["par2"])
            S.barrier()

        def fm_rstd(xt, sq, rstd, TT, xres, sqres, rres):
            act(sq[:, :, :TT], xt[:, :, :TT], AF.Square, [xres], [sqres])
            pi = nextps()
            for c in range(8):
                mm(pi, ps[pi][:, :TT], ones[:, :], sq[:, c, :TT], c == 0, c == 7, [sqres, "ones"])
            dve(lambda e: e.tensor_scalar(out=rstd[:, :TT], in0=ps[pi][:, :TT], scalar1=1.0 / D, scalar2=EPS, op0=ALU.mult, op1=ALU.add), [PSN[pi]], [rres])
            act(rstd[:, :TT], rstd[:, :TT], AF.Sqrt, [rres], [rres])
            dve(lambda e: e.reciprocal(out=rstd[:, :TT], in_=rstd[:, :TT]), [rres], [rres])

        def tiles_of(T, TTmax):
            TT = min(TTmax, T)
            return [(t0, TT) for t0 in range(0, T, TT)]

        def phase_T0():
            with contextlib.ExitStack() as ph:
                xin = [sb(ph, "t0x%d" % i, [128, D]) for i in range(2)]
                xo = [sb(ph, "t0o%d" % i, [128, 8, 128]) for i in range(2)]
                it = 0
                for g in "ps":
                    T = TG[g]
                    NT = min(128, T)
                    for j in range(T // NT):
                        b = it % 2; it += 1
                        ld(xin[b][:NT, :], x_in[g][j * NT:(j + 1) * NT, :], ["t0x%d" % b])
                        for c in range(8):
                            pi = 4 * b + c // 4
                            tr(pi, ps[pi][:, (c % 4) * 128:(c % 4) * 128 + NT], xin[b][:NT, c * 128:(c + 1) * 128], NT, ["t0x%d" % b])
                        for h in range(2):
                            pi = 4 * b + h
                            act(xo[b][:, 4 * h:4 * h + 4, :NT], ps[pi][:, :].rearrange("p (c t) -> p c t", t=128)[:, :, :NT], AF.Copy, [PSN[pi]], ["t0o%d" % b])
                        stg(xT[g].rearrange("(c p) t -> p c t", p=128)[:, :, j * NT:(j + 1) * NT], xo[b][:, :, :NT], ["t0o%d" % b])
                S.barrier()

        FBLK = [768 + 128 * i for i in range(6)] + [1800 + 128 * i for i in range(4)] + [2312 + 128 * i for i in range(4)]
        TGRP = [(256, 512, 0), (1536, 264, 512), (3848, 8, 776), (2312, 512, 784), (2824, 512, 1296), (3336, 512, 1808)]

        def phase_A(l):
            with contextlib.ExitStack() as ph:
                Win = sb(ph, "Win", [128, 8, NCOL], BF16)
                for c in range(8):
                    ld(Win[:, c, :], w_in[l, c * 128:(c + 1) * 128, :], ["Win"], q="pool")
                xt = sb(ph, "Axt", [128, 8, 512]); sq = sb(ph, "Asq", [128, 8, 512])
                rstd = sb(ph, "Arstd", [128, 512]); hT = sb(ph, "AhT", [128, 8, 512], BF16)
                PFt = sb(ph, "APFt", [128, 14, 512]); QKt = sb(ph, "AQKt", [128, 4, 512], BF16)
                PTt = sb(ph, "APTt", [128, 2320]); VAt = sb(ph, "AVAt", [128, 256], BF16)
                for g in "ps":
                    T = TG[g]
                    for (t0, TT) in tiles_of(T, 512):
                        ld(xt[:, :, :TT], xT[g].rearrange("(c p) t -> p c t", p=128)[:, :, t0:t0 + TT], ["Axt"])
                        fm_rstd(xt, sq, rstd, TT, "Axt", "Asq", "Arstd")
                        for c in range(8):
                            dve(lambda e: e.scalar_tensor_tensor(out=hT[:, c, :TT], in0=xt[:, c, :TT], scalar=gpre[:, c:c + 1], in1=rstd[:, :TT], op0=ALU.mult, op1=ALU.mult), ["Axt", "Arstd", "par"], ["AhT"])
                        for i in range(4):
                            pi = nextps()
                            for c in range(8):
                                mm(pi, ps[pi][:, :TT], Win[:, c, 128 * i:128 * i + 128], hT[:, c, :TT], c == 0, c == 7, ["Win", "AhT"])
                            act(QKt[:, i, :TT], ps[pi][:, :TT], AF.Copy, [PSN[pi]], ["AQKt"], scale=0.125 if i < 2 else 1.0)
                        stg(QAT[g].rearrange("(c p) t -> p c t", p=128)[:, :, t0:t0 + TT], QKt[:, 0:2, :TT], ["AQKt"])
                        stg(KAT[g].rearrange("(c p) t -> p c t", p=128)[:, :, PG[g] + t0:PG[g] + t0 + TT], QKt[:, 2:4, :TT], ["AQKt"])
                        for i, off in enumerate(FBLK):
                            pi = nextps()
                            for c in range(8):
                                mm(pi, ps[pi][:, :TT], Win[:, c, off:off + 128], hT[:, c, :TT], c == 0, c == 7, ["Win", "AhT"])
                            if i % 2 == 0:
                                act(PFt[:, i, :TT], ps[pi][:, :TT], AF.Copy, [PSN[pi]], ["APFt"])
                            else:
                                dve(lambda e: e.tensor_copy(out=PFt[:, i, :TT], in_=ps[pi][:, :TT]), [PSN[pi]], ["APFt"])
                        stg(PF[g].rearrange("(c p) t -> p c t", p=128)[:, :, t0:t0 + TT], PFt[:, :, :TT], ["APFt"])
                        NT = min(128, TT)
                        for j in range(TT // NT):
                            for gi, (woff, wn, poff) in enumerate(TGRP):
                                pi = nextps()
                                for c in range(8):
                                    mm(pi, ps[pi][:NT, :wn], hT[:, c, j * NT:(j + 1) * NT], Win[:, c, woff:woff + wn], c == 0, c == 7, ["Win", "AhT"])
                                if gi % 2 == 0:
                                    act(PTt[:NT, poff:poff + wn], ps[pi][:NT, :wn], AF.Copy, [PSN[pi]], ["APTt"])
                                else:
                                    dve(lambda e: e.tensor_copy(out=PTt[:NT, poff:poff + wn], in_=ps[pi][:NT, :wn]), [PSN[pi]], ["APTt"])
                            dve(lambda e: e.tensor_copy(out=VAt[:NT, :], in_=PTt[:NT, 256:512]), ["APTt"], ["AVAt"])
                            r0 = t0 + j * NT
                            stg(PT[g][r0:r0 + NT, :], PTt[:NT, :], ["APTt"])
                            stg(VA[g][PG[g] + r0:PG[g] + r0 + NT, :], VAt[:NT, :], ["AVAt"])
                            stg(o_sbk[g][l].rearrange("h t d -> t h d")[r0:r0 + NT], PTt[:NT, 0:256].rearrange("p (h d) -> p h d", d=64), ["APTt"])
                            stg(o_sbv[g][l].rearrange("h t d -> t h d")[r0:r0 + NT], PTt[:NT, 256:512].rearrange("p (h d) -> p h d", d=64), ["APTt"])
                S.barrier()

        def phase_attn(l, g):
            T = TG[g]; P0 = PG[g]; Tk = P0 + T
            NKB = (Tk + 127) // 128
            with contextlib.ExitStack() as ph:
                Kt = sb(ph, "atK", [128, 2, Tk], BF16)
                Vt = sb(ph, "atV", [128, NKB, 256], BF16)
                Qt = sb(ph, "atQ", [128, 2, 512], BF16)
                Qn = sb(ph, "atQn", [128, 2, 512], BF16)
                et = [sb(ph, "ate%d" % i, [128, 512]) for i in range(2)]
                spt = [sb(ph, "atsp%d" % i, [128, 512]) for i in range(2)]
                sps = [sb(ph, "atss%d" % i, [128, 512]) for i in range(2)]
                At = [sb(ph, "atA%d" % i, [128, 512], BF16) for i in range(2)]
                ost = [sb(ph, "ato%d" % i, [64, 512], BF16) for i in range(2)]
                if P0 > 0:
                    ckt = sb(ph, "atck", [128, 256])
                    for kb in range(P0 // 128):
                        for h4 in range(4):
                            ld(ckt[:, 64 * h4:64 * h4 + 64], cache_k[l, h4, kb * 128:(kb + 1) * 128, :], ["atck"])
                        pi = nextps()
                        for hp in range(2):
                            tr(pi, ps[pi][:, hp * 128:(hp + 1) * 128], ckt[:, hp * 128:(hp + 1) * 128], 128, ["atck"])
                        act(Kt[:, :, kb * 128:(kb + 1) * 128], ps[pi][:, 0:256].rearrange("p (c t) -> p c t", t=128), AF.Copy, [PSN[pi]], ["atK"])
                    for kb in range(P0 // 128):
                        for h4 in range(4):
                            ld(ckt[:, 64 * h4:64 * h4 + 64], cache_v[l, h4, kb * 128:(kb + 1) * 128, :], ["atck"])
                        dve(lambda e: e.tensor_copy(out=Vt[:, kb, :], in_=ckt[:, :]), ["atck"], ["atV"])
                    ld(Kt[:, :, P0:Tk], KAT[g].rearrange("(c p) t -> p c t", p=128)[:, :, P0:Tk], ["atK"])
                    ld(Vt[:T, P0 // 128, :], VA[g][P0:Tk, :], ["atV"])
                else:
                    ld(Kt[:, :, :], KAT[g].rearrange("(c p) t -> p c t", p=128), ["atK"])
                    ld(Vt[:, :, :], VA[g].rearrange("(b p) c -> p b c", p=128), ["atV"])
                if g == "s" and KSUB <= 1:
                    S.barrier()
                    return
                def cut(level, hh_=0):
                    if g == "s" and KSUB == level and hh_ == KHH:
                        raise _Cut()
                try:
                  for (q0, Nq) in tiles_of(T, 512):
                      ld(Qt[:, :, :Nq], QAT[g].rearrange("(c p) t -> p c t", p=128)[:, :, q0:q0 + Nq], ["atQ"])
                      dve(lambda e: e.tensor_scalar(out=Qn[:, :, :Nq], in0=Qt[:, :, :Nq], scalar1=-1.0, scalar2=None, op0=ALU.mult), ["atQ"], ["atQn"])
                      qa0 = P0 + q0
                      kbs = list(range((qa0 + Nq + 127) // 128 - 1, -1, -1))
                      for hp in range(2):
                          for n, kb in enumerate(kbs):
                              k0 = kb * 128
                              nk = min(128, Tk - k0)
                              diag = (k0 + nk > qa0)
                              jm = (k0 - qa0) // 128 if diag else 0
                              for hh in range(2):
                                  h = 2 * hp + hh
                                  po = 64 * hh
                                  zb, rb, ob = hh, 2 + hh, 4 + hh
                                  R = lambda s: s + str(hh)
                                  Kh = Kt[po:po + 64, hp, k0:k0 + nk]
                                  Qh = Qt[po:po + 64, hp, :Nq]
                                  mm(zb, ps[zb][:nk, :Nq], Kh, Qh, True, True, ["atK", "atQ"])
                                  act(et[hh][:nk, :Nq], ps[zb][:nk, :Nq], AF.Exp, [PSN[zb]], [R("ate")])
                                  act(spt[hh][:nk, :Nq], et[hh][:nk, :Nq], AF.Ln, [R("ate")], [R("atsp")], bias=1.0)
                                  if diag:
                                      dve(lambda e: e.tensor_tensor(out=spt[hh][:nk, :Nq], in0=spt[hh][:nk, :Nq], in1=amask[:nk, jm, :Nq], op=ALU.mult), [R("atsp"), "amask"], [R("atsp")])
                                  cut(2, hh)
                                  if nk < 128:
                                      pool(lambda e: e.memset(spt[hh][nk:128, :Nq], 0.0), [], [R("atsp")])
                                  mm(rb, ps[rb][:nk, :Nq], Uge[:, :nk], spt[hh][:, :Nq], True, False, [R("atsp"), "Uge"])
                                  if n > 0:
                                      mm(rb, ps[rb][:nk, :Nq], ones[:, :nk], sps[hh][:, :Nq], False, False, [R("atss"), "ones"])
                                  mm(rb, ps[rb][:nk, :Nq], Kh, Qn[po:po + 64, hp, :Nq], False, True, ["atK", "atQn"])
                                  act(At[hh][:nk, :Nq], ps[rb][:nk, :Nq], AF.Exp, [PSN[rb]], [R("atA")], scale=-1.0)
                                  if diag:
                                      dve(lambda e: e.tensor_tensor(out=At[hh][:nk, :Nq], in0=At[hh][:nk, :Nq], in1=amask[:nk, jm, :Nq], op=ALU.mult), [R("atA"), "amask"], [R("atA")])
                                  cut(3, hh)
                                  if n == 0:
                                      if nk < 128:
                                          pool(lambda e: e.memset(sps[hh][:, :Nq], 0.0), [], [R("atss")])
                                      pool(lambda e: e.tensor_copy(out=sps[hh][:nk, :Nq], in_=spt[hh][:nk, :Nq]), [R("atsp")], [R("atss")])
                                  elif n < len(kbs) - 1:
                                      pool(lambda e: e.tensor_tensor(out=sps[hh][:nk, :Nq], in0=sps[hh][:nk, :Nq], in1=spt[hh][:nk, :Nq], op=ALU.add), [R("atsp"), R("atss")], [R("atss")])
                                  cut(4, hh)
                                  mm(ob, ps[ob][:64, :Nq], Vt[:nk, kb, 64 * h:64 * h + 64], At[hh][:nk, :Nq], n == 0, n == len(kbs) - 1, ["atV", R("atA")])
                                  cut(5, hh)
                              cut(6)
                          cut(7)
                          for hh in range(2):
                              h = 2 * hp + hh
                              ob = 4 + hh
                              dve(lambda e: e.tensor_copy(out=ost[hh][:, :Nq], in_=ps[ob][:64, :Nq]), [PSN[ob]], ["ato%d" % hh])
                              stg(mixT[g][64 * h:64 * h + 64, q0:q0 + Nq], ost[hh][:, :Nq], ["ato%d" % hh])
                except _Cut:
                    pass
                S.barrier()

        BD2 = sb(st, "BD2", [128, 128])
        pool(lambda e: e.memset(BD2[:], 1.0), [], ["BD2"])
        pool(lambda e: e.memset(BD2[0:64, 64:128], 0.0), [], ["BD2"])
        pool(lambda e: e.memset(BD2[64:128, 0:64], 0.0), [], ["BD2"])
        NLT64 = sb(st, "NLT64", [64, 64]); NEGM64 = sb(st, "NEGM64", [64, 64])
        dve(lambda e: e.tensor_scalar(out=NLT64[:, :], in0=LT64[:, :], scalar1=-1.0, scalar2=None, op0=ALU.mult), ["LT64"], ["NLT64"])
        dve(lambda e: e.tensor_scalar(out=NEGM64[:, :], in0=LT64[:, :], scalar1=1e30, scalar2=-1e30, op0=ALU.mult, op1=ALU.add), ["LT64"], ["NEGM64"])
        CM = sb(st, "CM", [64, 12, 64])
        ld(CM[:, :, :], cmask_in.rearrange("m p f -> p m f"), ["CM"])
        I64 = ident[:64, :64]
        O64 = ones[:64, :64]

        def small_rstd(ss, n, tag):
            dve(lambda e: e.tensor_scalar(out=ss, in0=ss, scalar1=1.0 / n, scalar2=EPS, op0=ALU.mult, op1=ALU.add), [tag], [tag])
            act(ss, ss, AF.Sqrt, [tag], [tag])
            dve(lambda e: e.reciprocal(out=ss, in_=ss), [tag], [tag])

        def phase_gdn(l, g):
            T = TG[g]
            with contextlib.ExitStack() as ph:
                X = sb(ph, "gX", [128, 6, 3 + 512]); Y = sb(ph, "gY", [128, 6, 512])
                sq = sb(ph, "gsq", [128, 4, 512]); rs = sb(ph, "grs", [128, 4, 512])
                Sst = sb(ph, "gS", [128, 2, 64])
                ptc = sb(ph, "gptc", [64, 264]); ktv = sb(ph, "gktv", [64, 512])
                gt = sb(ph, "ggt", [64, 40]); gcl = sb(ph, "ggcl", [128, 8])
                def L4(nm, shape=(64, 64)):
                    return [sb(ph, nm + str(i), list(shape)) for i in range(4)]
                dgs = L4("gdg"); a1s = L4("ga1"); a2s = L4("ga2"); gSLs = L4("ggSL"); gUTs = L4("ggUT")
                Pms = [[sb(ph, "gP%d_%d" % (h_, i), [64, 64]) for i in range(2)] for h_ in range(4)]
                PTms = [[sb(ph, "gPT%d_%d" % (h_, i), [64, 64]) for i in range(2)] for h_ in range(4)]
                Yms = [[sb(ph, "gYm%d_%d" % (h_, i), [64, 64]) for i in range(2)] for h_ in range(4)]
                Xms = [[sb(ph, "gXm%d_%d" % (h_, i), [64, 64]) for i in range(2)] for h_ in range(4)]
                Rus = L4("gRu"); Rws = L4("gRw"); usbs = L4("gusb")
                Abs_ = L4("gAb"); Nbs = L4("gNb"); Z1s = L4("gZ1"); Y1s = L4("gY1")
                wTs = L4("gwT", (128, 64)); vnews = L4("gvn"); o1s = L4("go1")
                qkms = L4("gqkm"); kdecs = L4("gkdec"); osqs = L4("gosq")
                ss = sb(ph, "gss", [64, 4]); zs = sb(ph, "gzs", [64, 256]); omix = sb(ph, "gomix", [64, 256])
                mst = sb(ph, "gmst", [128, 2, 64], BF16)
                Sv = "(hp hh) k v -> (hh k) hp v"
                GSN = ["gS%d" % i for i in range(4)]
                if g == "s":
                    ld(Sst[:, :, :], st_gs[l].rearrange(Sv, hh=2), GSN)
                    for j in range(3):
                        ld(X[:, :, j], st_gconv[l, j].rearrange("(b p) -> p b", p=128), ["gX"], allow_slow_non_contiguous=True)
                else:
                    dve(lambda e: e.memset(Sst[:], 0.0), [], GSN)
                    dve(lambda e: e.memset(X[:, :, 0:3], 0.0), [], ["gX"])
                tl = tiles_of(T, 512)
                for ti, (t0, TT) in enumerate(tl):
                    ld(X[:, :, 3:3 + TT], PF[g].rearrange("(c p) t -> p c t", p=128)[:, 0:6, t0:t0 + TT], ["gX"])
                    for b in range(6):
                        en = dve
                        en(lambda e: e.tensor_scalar(out=Y[:, b, :TT], in0=X[:, b, 0:TT], scalar1=wcg[:, b, 0:1], scalar2=None, op0=ALU.mult), ["gX", "par"], ["gY%d" % b])
                        for j in range(1, 4):
                            en(lambda e: e.scalar_tensor_tensor(out=Y[:, b, :TT], in0=X[:, b, j:j + TT], scalar=wcg[:, b, j:j + 1], in1=Y[:, b, :TT], op0=ALU.mult, op1=ALU.add), ["gX", "par", "gY%d" % b], ["gY%d" % b])
                    if ti == len(tl) - 1:
                        for j in range(3):
                            stg(o_gconv[g][l, j].rearrange("(b p) -> p b", p=128), X[:, :, TT + j], ["gX"], allow_slow_non_contiguous=True)
                    YR = ["gY%d" % b for b in range(6)]
                    dve(lambda e: e.tensor_copy(out=X[:, :, 0:3], in_=X[:, :, TT:TT + 3]), ["gX"] + YR, ["gX"])
                    act(Y[:, :, :TT], Y[:, :, :TT], AF.Silu, YR, YR)
                    act(sq[:, :, :TT], Y[:, 0:4, :TT], AF.Square, YR, ["gsq"])
                    for b in range(4):
                        pi = nextps()
                        mm(pi, ps[pi][:, :TT], BD2[:, :], sq[:, b, :TT], True, True, ["gsq", "BD2"])
                        dve(lambda e: e.tensor_scalar(out=rs[:, b, :TT], in0=ps[pi][:, :TT], scalar1=EPS, scalar2=None, op0=ALU.add), [PSN[pi]], ["grs"])
                    act(rs[:, :, :TT], rs[:, :, :TT], AF.Sqrt, ["grs"], ["grs"])
                    dve(lambda e: e.reciprocal(out=rs[:, :, :TT], in_=rs[:, :, :TT]), ["grs"], ["grs"])
                    for b in range(4):
                        dve(lambda e: e.scalar_tensor_tensor(out=Y[:, b, :TT], in0=Y[:, b, :TT], scalar=0.125 if b < 2 else 1.0, in1=rs[:, b, :TT], op0=ALU.mult, op1=ALU.mult), ["grs"] + YR, YR)
                    for c in range(TT // 64):
                        cs = slice(c * 64, c * 64 + 64)
                        r0 = t0 + c * 64
                        ld(ptc[:, :], PT[g][r0:r0 + 64, 512:776], ["gptc"])
                        pi = nextps()
                        for i, b in enumerate((2, 3, 4, 5)):
                            tr(pi, ps[pi][:64, i * 128:(i + 1) * 128], Y[:, b, cs], 128, YR)
                        act(ktv[:, :], ps[pi][:64, :], AF.Copy, [PSN[pi]], ["gktv"])
                        act(gt[:, 0:4], ptc[:, 256:260], AF.Sigmoid, ["gptc"], ["ggt"])
                        dve(lambda e: e.tensor_tensor(out=gt[:, 24:28], in0=ptc[:, 260:264], in1=dtb[:64, :], op=ALU.add), ["gptc", "par"], ["ggt"])
                        act(gt[:, 24:28], gt[:, 24:28], AF.Exp, ["ggt"], ["ggt"])
                        act(gt[:, 24:28], gt[:, 24:28], AF.Ln, ["ggt"], ["ggt"], bias=1.0)
                        dve(lambda e: e.tensor_tensor(out=gt[:, 4:8], in0=gt[:, 24:28], in1=nexpa[:64, :], op=ALU.mult), ["ggt", "par2"], ["ggt"])
                        pg = nextps()
                        mm(pg, ps[pg][:64, 0:4], UT64[:, :], gt[:, 4:8], True, True, ["ggt", "UT64"])
                        mm(pg, ps[pg][:, 4:8], ones[:64, :], gt[:, 4:8], True, True, ["ggt", "ones"])
                        dve(lambda e: e.tensor_copy(out=gcl[:, 4:8], in_=ps[pg][:, 4:8]), [PSN[pg]], ["ggcl"])
                        dve(lambda e: e.tensor_copy(out=gcl[:64, 0:4], in_=ps[pg][:64, 0:4]), [PSN[pg]], ["ggcl"])
                        act(gt[:, 8:12], gcl[:64, 0:4], AF.Exp, ["ggcl"], ["ggt"])
                        dve(lambda e: e.tensor_tensor(out=gt[:, 12:16], in0=gcl[:64, 4:8], in1=gcl[:64, 0:4], op=ALU.subtract), ["ggcl"], ["ggt"])
                        act(gt[:, 12:16], gt[:, 12:16], AF.Exp, ["ggt"], ["ggt"])
                        act(gcl[:, 4:8], gcl[:, 4:8], AF.Exp, ["ggcl"], ["ggcl"])
                        dve(lambda e: e.tensor_scalar(out=gt[:, 16:20], in0=gt[:, 0:4], scalar1=-1.0, scalar2=None, op0=ALU.mult), ["ggt"], ["ggt"])
                        dve(lambda e: e.tensor_tensor(out=gt[:, 20:24], in0=gt[:, 0:4], in1=gt[:, 8:12], op=ALU.mult), ["ggt"], ["ggt"])
                        act(zs[:, :], ptc[:, 0:256], AF.Silu, ["gptc"], ["gzs"])
                        def ghead(h):
                            H = str(h)
                            dg = dgs[h]; a1 = a1s[h]; a2 = a2s[h]; gSL = gSLs[h]; gUT = gUTs[h]
                            Pm = Pms[h]; PTm = PTms[h]; Ym = Yms[h]; Xm = Xms[h]
                            Ab = Abs_[h]; Nb = Nbs[h]; Z1 = Z1s[h]; Y1 = Y1s[h]; Ru = Rus[h]; Rw = Rws[h]; usb = usbs[h]
                            wT = wTs[h]; vnew = vnews[h]; o1 = o1s[h]; qkm = qkms[h]; kdec = kdecs[h]; osq = osqs[h]
                            hp, po = h // 2, 64 * (h % 2)
                            QnT = Y[po:po + 64, hp, cs]; KnT = Y[po:po + 64, 2 + hp, cs]
                            kn = ktv[:, 64 * h:64 * h + 64]; vv = ktv[:, 256 + 64 * h:256 + 64 * h + 64]
                            Sh = Sst[po:po + 64, hp, :]
                            dve(lambda e: e.tensor_scalar(out=dg[:, :], in0=I64, scalar1=gcl[:64, h:h + 1], scalar2=None, op0=ALU.mult), ["ident", "ggcl"], [("gdg" + H)])
                            yield
                            pD = nextps()
                            mm(pD, ps[pD][:64, :64], O64, dg[:, :], True, True, [("gdg" + H), "ones"])
                            yield
                            dve(lambda e: e.scalar_tensor_tensor(out=a1[:, :], in0=ps[pD][:64, :64], scalar=gcl[:64, h:h + 1], in1=LT64[:, :], op0=ALU.subtract, op1=ALU.mult), [PSN[pD], "ggcl", "LT64"], [("ga1" + H)])
                            yield
                            dve(lambda e: e.scalar_tensor_tensor(out=a2[:, :], in0=ps[pD][:64, :64], scalar=gcl[:64, h:h + 1], in1=UT64[:, :], op0=ALU.subtract, op1=ALU.mult), [PSN[pD], "ggcl", "UT64"], [("ga2" + H)])
                            yield
                            act(a1[:, :], a1[:, :], AF.Exp, [("ga1" + H)], [("ga1" + H)], scale=-1.0)
                            yield
                            act(a2[:, :], a2[:, :], AF.Exp, [("ga2" + H)], [("ga2" + H)])
                            yield
                            dve(lambda e: e.tensor_tensor(out=gSL[:, :], in0=a1[:, :], in1=SL64[:, :], op=ALU.mult), [("ga1" + H), "SL64"], [("ggSL" + H)])
                            yield
                            dve(lambda e: e.tensor_tensor(out=gUT[:, :], in0=a2[:, :], in1=UT64[:, :], op=ALU.mult), [("ga2" + H), "UT64"], [("ggUT" + H)])
                            yield
                            pK = nextps()
                            mm(pK, ps[pK][:64, :64], KnT, KnT, True, True, YR)
                            yield
                            dve(lambda e: e.scalar_tensor_tensor(out=PTm[0][:, :], in0=ps[pK][:64, :64], scalar=gt[:, 16 + h:17 + h], in1=gSL[:, :], op0=ALU.mult, op1=ALU.mult), [PSN[pK], "ggt", ("ggSL" + H)], [("gPT" + H + "_0")])
                            yield
                            pA = nextps()
                            tr(pA, ps[pA][:64, :64], PTm[0][:, :], 64, [("gPT" + H + "_0")])
                            yield
                            act(Pm[0][:, :], ps[pA][:64, :64], AF.Copy, [PSN[pA]], [("gP" + H + "_0")])
                            yield
                            A0 = Pm[0]; N0 = PTm[0]
                            pool(lambda e: e.tensor_tensor(out=Ym[0][:, :], in0=A0[:, :], in1=CM[:, 0, :], op=ALU.mult), [("gP" + H + "_0"), "CM"], [("gYm" + H + "_0")])
                            yield
                            pool(lambda e: e.tensor_tensor(out=Xm[0][:, :], in0=N0[:, :], in1=CM[:, 1, :], op=ALU.mult), [("gPT" + H + "_0"), "CM"], [("gXm" + H + "_0")])
                            yield
                            dve(lambda e: e.tensor_tensor(out=Ym[0][:, :], in0=Ym[0][:, :], in1=I64, op=ALU.add), [("gYm" + H + "_0"), "ident"], [("gYm" + H + "_0")])
                            yield
                            dve(lambda e: e.tensor_tensor(out=Xm[0][:, :], in0=Xm[0][:, :], in1=I64, op=ALU.add), [("gXm" + H + "_0"), "ident"], [("gXm" + H + "_0")])
                            yield
                            cur = 0
                            for lv in range(1, 6):
                                nx = 1 - cur
                                Wc, Xc = Ym[cur], Xm[cur]
                                Wn_, Xn_ = ("gYm" + H + "_%d") % cur, ("gXm" + H + "_%d") % cur
                                pool(lambda e: e.tensor_tensor(out=Ab[:, :], in0=A0[:, :], in1=CM[:, 2 * lv, :], op=ALU.mult), [("gP" + H + "_0"), "CM"], [("gAb" + H)])
                                yield
                                pool(lambda e: e.tensor_tensor(out=Nb[:, :], in0=N0[:, :], in1=CM[:, 2 * lv + 1, :], op=ALU.mult), [("gPT" + H + "_0"), "CM"], [("gNb" + H)])
                                yield
                                p1 = nextps()
                                mm(p1, ps[p1][:64, :64], Nb[:, :], Wc[:, :], True, True, [("gNb" + H), Wn_])
                                yield
                                act(Z1[:, :], ps[p1][:64, :64], AF.Copy, [PSN[p1]], [("gZ1" + H)])
                                yield
                                p2 = nextps()
                                mm(p2, ps[p2][:64, :64], Xc[:, :], Z1[:, :], True, True, [Xn_, ("gZ1" + H)])
                                yield
                                dve(lambda e: e.tensor_tensor(out=Ym[nx][:, :], in0=ps[p2][:64, :64], in1=Wc[:, :], op=ALU.add), [PSN[p2], Wn_], [("gYm" + H + "_%d") % nx])
                                yield
                                if lv < 5:
                                    p3 = nextps()
                                    mm(p3, ps[p3][:64, :64], Ab[:, :], Xc[:, :], True, True, [("gAb" + H), Xn_])
                                    yield
                                    act(Y1[:, :], ps[p3][:64, :64], AF.Copy, [PSN[p3]], [("gY1" + H)])
                                    yield
                                    p4 = nextps()
                                    mm(p4, ps[p4][:64, :64], Wc[:, :], Y1[:, :], True, True, [Wn_, ("gY1" + H)])
                                    yield
                                    dve(lambda e: e.tensor_tensor(out=Xm[nx][:, :], in0=ps[p4][:64, :64], in1=Xc[:, :], op=ALU.add), [PSN[p4], Xn_], [("gXm" + H + "_%d") % nx])
                                    yield
                                cur = nx
                            TTm = Ym[cur]; TTn = ("gYm" + H + "_%d") % cur
                            dve(lambda e: e.tensor_scalar(out=Ru[:, :], in0=vv, scalar1=gt[:, h:h + 1], scalar2=None, op0=ALU.mult), ["gktv", "ggt"], [("gRu" + H)])
                            yield
                            dve(lambda e: e.tensor_scalar(out=Rw[:, :], in0=kn, scalar1=gt[:, 20 + h:21 + h], scalar2=None, op0=ALU.mult), ["gktv", "ggt"], [("gRw" + H)])
                            yield
                            pu = nextps(); pw = nextps()
                            mm(pu, ps[pu][:64, :64], TTm[:, :], Ru[:, :], True, True, [TTn, ("gRu" + H)])
                            yield
                            mm(pw, ps[pw][:64, :64], Rw[:, :], TTm[:, :], True, True, [TTn, ("gRw" + H)])
                            yield
                            act(usb[:, :], ps[pu][:64, :64], AF.Copy, [PSN[pu]], [("gusb" + H)])
                            yield
                            dve(lambda e: e.tensor_copy(out=wT[po:po + 64, :], in_=ps[pw][:64, :64]), [PSN[pw]], [("gwT" + H)])
                            yield
                            pS = nextps()
                            mm(pS, ps[pS][:64, :64], wT[po:po + 64, :], Sh, True, True, [("gwT" + H), ("gS" + H)])
                            yield
                            dve(lambda e: e.tensor_tensor(out=vnew[:, :], in0=usb[:, :], in1=ps[pS][:64, :64], op=ALU.subtract), [("gusb" + H), PSN[pS]], [("gvn" + H)])
                            yield
                            pq = nextps()
                            mm(pq, ps[pq][:64, :64], QnT, Sh, True, True, YR + [("gS" + H)])
                            yield
                            dve(lambda e: e.tensor_scalar(out=o1[:, :], in0=ps[pq][:64, :64], scalar1=gt[:, 8 + h:9 + h], scalar2=None, op0=ALU.mult), [PSN[pq], "ggt"], [("go1" + H)])
                            yield
                            pk2 = nextps()
                            mm(pk2, ps[pk2][:64, :64], KnT, QnT, True, True, YR)
                            yield
                            dve(lambda e: e.tensor_tensor(out=qkm[:, :], in0=ps[pk2][:64, :64], in1=gUT[:, :], op=ALU.mult), [PSN[pk2], ("ggUT" + H)], [("gqkm" + H)])
                            yield
                            po2 = nextps()
                            mm(po2, ps[po2][:64, :64], qkm[:, :], vnew[:, :], True, True, [("gqkm" + H), ("gvn" + H)])
                            yield
                            dve(lambda e: e.tensor_tensor(out=o1[:, :], in0=o1[:, :], in1=ps[po2][:64, :64], op=ALU.add), [("go1" + H), PSN[po2]], [("go1" + H)])
                            yield
                            dve(lambda e: e.tensor_scalar(out=kdec[:, :], in0=kn, scalar1=gt[:, 12 + h:13 + h], scalar2=None, op0=ALU.mult), ["gktv", "ggt"], [("gkdec" + H)])
                            yield
                            pU = nextps()
                            mm(pU, ps[pU][:64, :64], kdec[:, :], vnew[:, :], True, True, [("gkdec" + H), ("gvn" + H)])
                            yield
                            dve(lambda e: e.scalar_tensor_tensor(out=Sh, in0=Sh, scalar=gcl[po:po + 64, 4 + h:5 + h], in1=ps[pU][:64, :64], op0=ALU.mult, op1=ALU.add), [("gS" + H), "ggcl", PSN[pU]], [("gS" + H)])
                            yield
                            dve(lambda e: e.tensor_tensor(out=osq[:, :], in0=o1[:, :], in1=o1[:, :], op=ALU.mult), [("go1" + H)], [("gosq" + H)])
                            yield
                            dve(lambda e: e.tensor_reduce(out=ss[:, h:h + 1], in_=osq[:, :], axis=AX.X, op=ALU.add), [("gosq" + H)], [("gss" + H)])
                            yield
                            small_rstd(ss[:, h:h + 1], 64, ("gss" + H))
                            yield
                            dve(lambda e: e.scalar_tensor_tensor(out=o1[:, :], in0=o1[:, :], scalar=ss[:, h:h + 1], in1=gng[:64, :], op0=ALU.mult, op1=ALU.mult), [("go1" + H), ("gss" + H), "par"], [("go1" + H)])
                            yield
                            dve(lambda e: e.tensor_tensor(out=omix[:, 64 * h:64 * h + 64], in0=o1[:, :], in1=zs[:, 64 * h:64 * h + 64], op=ALU.mult), [("go1" + H), "gzs"], ["gomix"])
                            yield
                        gens = [ghead(h) for h in range(4)]
                        while gens:
                            for gen in list(gens):
                                try:
                                    next(gen)
                                except StopIteration:
                                    gens.remove(gen)
                        pm = nextps()
                        for i in range(2):
                            tr(pm, ps[pm][:, i * 64:(i + 1) * 64], omix[:, i * 128:(i + 1) * 128], 64, ["gomix"])
                        act(mst[:, :, :], ps[pm][:, 0:128].rearrange("p (c t) -> p c t", t=64), AF.Copy, [PSN[pm]], ["gmst"])
                        stg(mixT[g].rearrange("(c p) t -> p c t", p=128)[:, 2:4, r0:r0 + 64], mst[:, :, :], ["gmst"])
                stg(o_gs[g][l].rearrange(Sv, hh=2), Sst[:, :, :], GSN)
                S.barrier()

        def phase_mlstm(l, g):
            T = TG[g]
            SC = 128 ** -0.5
            with contextlib.ExitStack() as ph:
                QK = sb(ph, "mQK", [128, 8, 512])
                Cn = sb(ph, "mCn", [128, 4, 129]); mb = sb(ph, "mmb", [128, 4])
                ptc = sb(ph, "mptc", [64, 1544]); v1 = sb(ph, "mv1", [64, 4, 129])
                gt = sb(ph, "mgt", [64, 48]); g128 = sb(ph, "mg128", [128, 16])
                def L4(nm, shape=(64, 64)):
                    return [sb(ph, nm + str(i), list(shape)) for i in range(4)]
                dgs = L4("mdg"); dms = L4("mdm"); wms = L4("mwm"); qks = L4("mqk"); qkTs = L4("mqkT")
                t1s = L4("mt1", (64, 129)); tots = L4("mtot", (64, 129)); kws = L4("mkw", (64, 128))
                hss = L4("mhs", (64, 128)); hsqs = L4("mhsq", (64, 128)); ghs = L4("mgh", (64, 8)); g1s = L4("mg1", (128, 2))
                MCN = ["mCn%d" % i for i in range(4)]; MMBN = ["mmb%d" % i for i in range(4)]
                sg = sb(ph, "msg", [64, 512])
                hmix = sb(ph, "mhmix", [64, 512]); mst = sb(ph, "mmst", [128, 4, 64], BF16)
                if g == "s":
                    ld(Cn[:, :, 0:128], st_mc[l].rearrange("h k v -> k h v"), MCN)
                    ld(Cn[:, :, 128:129], st_mn[l].rearrange("h (k o) -> k h o", o=1), MCN, allow_slow_non_contiguous=True)
                    ld(mb[:, :], st_mm[l].partition_broadcast(128), MMBN)
                else:
                    dve(lambda e: e.memset(Cn[:], 0.0), [], MCN)
                    dve(lambda e: e.memset(mb[:], NEG), [], MMBN)
                dve(lambda e: e.memset(v1[:, :, 128:129], 1.0), [], ["mv1"])
                for (t0, TT) in tiles_of(T, 512):
                    ld(QK[:, :, :TT], PF[g].rearrange("(c p) t -> p c t", p=128)[:, 6:14, t0:t0 + TT], ["mQK"])
                    for c in range(TT // 64):
                        cs = slice(c * 64, c * 64 + 64)
                        r0 = t0 + c * 64
                        ld(ptc[:, :], PT[g][r0:r0 + 64, 776:2320], ["mptc"])
                        dve(lambda e: e.tensor_copy(out=v1[:, :, 0:128], in_=ptc[:, 520:1032].rearrange("p (h d) -> p h d", d=128)), ["mptc"], ["mv1"])
                        act(sg[:, :], ptc[:, 1032:1544], AF.Sigmoid, ["mptc"], ["msg"])
                        dve(lambda e: e.tensor_tensor(out=gt[:, 0:4], in0=ptc[:, 0:4], in1=bi_b[:64, :], op=ALU.add), ["mptc", "par"], ["mgt"])
                        dve(lambda e: e.tensor_tensor(out=gt[:, 4:8], in0=ptc[:, 4:8], in1=bf_b[:64, :], op=ALU.add), ["mptc", "par"], ["mgt"])
                        act(gt[:, 4:8], gt[:, 4:8], AF.Exp, ["mgt"], ["mgt"], scale=-1.0)
                        act(gt[:, 4:8], gt[:, 4:8], AF.Ln, ["mgt"], ["mgt"], bias=1.0)
                        dve(lambda e: e.tensor_scalar(out=gt[:, 4:8], in0=gt[:, 4:8], scalar1=-1.0, scalar2=None, op0=ALU.mult), ["mgt"], ["mgt"])
                        pg = nextps()
                        mm(pg, ps[pg][:64, 0:4], UT64[:, :], gt[:, 4:8], True, True, ["mgt", "UT64"])
                        mm(pg, ps[pg][:, 4:8], ones[:64, :], gt[:, 4:8], True, True, ["mgt", "ones"])
                        dve(lambda e: e.tensor_copy(out=gt[:, 8:12], in_=ps[pg][:64, 0:4]), [PSN[pg]], ["mgt"])
                        dve(lambda e: e.tensor_copy(out=g128[:, 0:4], in_=ps[pg][:, 4:8]), [PSN[pg]], ["mg128"])
                        dve(lambda e: e.tensor_tensor(out=gt[:, 12:16], in0=gt[:, 8:12], in1=gt[:, 0:4], op=ALU.subtract), ["mgt"], ["mgt"])
                        dve(lambda e: e.tensor_tensor(out=gt[:, 16:20], in0=gt[:, 8:12], in1=mb[:64, :], op=ALU.add), ["mgt"] + MMBN, ["mgt"])
                        def mhead(h):
                            H = str(h); MG = 'mgh' + H; M1 = 'mg1' + H
                            gh = ghs[h]; g1 = g1s[h]
                            dg = dgs[h]; dm = dms[h]; wm = wms[h]; qk = qks[h]; qkT = qkTs[h]; t1 = t1s[h]; tot = tots[h]; kw = kws[h]; hs = hss[h]; hsq = hsqs[h]
                            QT = QK[:, h, cs]; KT = QK[:, 4 + h, cs]
                            kk = ptc[:, 8 + 128 * h:8 + 128 * h + 128]
                            R = lambda s: s
                            dve(lambda e: e.tensor_scalar(out=dg[:, :], in0=I64, scalar1=gt[:, 12 + h:13 + h], scalar2=None, op0=ALU.mult), ["ident", "mgt", MG], [("mdg" + H)])
                            yield
                            pD = nextps()
                            mm(pD, ps[pD][:64, :64], O64, dg[:, :], True, True, [("mdg" + H), "ones"])
                            yield
                            dve(lambda e: e.scalar_tensor_tensor(out=dm[:, :], in0=ps[pD][:64, :64], scalar=gt[:, 8 + h:9 + h], in1=NLT64[:, :], op0=ALU.subtract, op1=ALU.mult), [PSN[pD], "mgt", MG, "NLT64"], [("mdm" + H)])
                            yield
                            dve(lambda e: e.tensor_tensor(out=dm[:, :], in0=dm[:, :], in1=NEGM64[:, :], op=ALU.add), [("mdm" + H), "NEGM64"], [("mdm" + H)])
                            yield
                            dve(lambda e: e.tensor_reduce(out=gh[:, 6:7], in_=dm[:, :], axis=AX.X, op=ALU.max), [("mdm" + H)], [MG])
                            yield
                            dve(lambda e: e.tensor_tensor(out=gh[:, 0:1], in0=gh[:, 6:7], in1=gt[:, 16 + h:17 + h], op=ALU.max), ["mgt", MG], [MG])
                            yield
                            dve(lambda e: e.tensor_scalar(out=gh[:, 2:3], in0=gh[:, 0:1], scalar1=-1.0, scalar2=None, op0=ALU.mult), ["mgt", MG], [MG])
                            yield
                            act(wm[:, :], dm[:, :], AF.Exp, [("mdm" + H), "mgt", MG], [("mwm" + H)], bias=gh[:, 2:3])
                            yield
                            act(gh[:, 1:2], gt[:, 16 + h:17 + h], AF.Exp, ["mgt", MG], [MG], bias=gh[:, 2:3])
                            yield
                            pq = nextps()
                            mm(pq, ps[pq][:64, :64], QT, KT, True, True, ["mQK"])
                            yield
                            dve(lambda e: e.scalar_tensor_tensor(out=qk[:, :], in0=ps[pq][:64, :64], scalar=SC, in1=wm[:, :], op0=ALU.mult, op1=ALU.mult), [PSN[pq], ("mwm" + H)], [("mqk" + H)])
                            yield
                            pt = nextps()
                            tr(pt, ps[pt][:64, :64], qk[:, :], 64, [("mqk" + H)])
                            yield
                            act(qkT[:, :], ps[pt][:64, :64], AF.Copy, [PSN[pt]], [("mqkT" + H)])
                            yield
                            pc = nextps(); pv = nextps()
                            mm(pc, ps[pc][:64, :129], QT, Cn[:, h, :], True, True, ["mQK", ("mCn" + H)])
                            yield
                            mm(pv, ps[pv][:64, :129], qkT[:, :], v1[:, h, :], True, True, [("mqkT" + H), "mv1"])
                            yield
                            dve(lambda e: e.tensor_scalar(out=t1[:, :], in0=ps[pc][:64, :129], scalar1=gh[:, 1:2], scalar2=None, op0=ALU.mult), [PSN[pc], "mgt", MG], [("mt1" + H)])
                            yield
                            dve(lambda e: e.tensor_tensor(out=tot[:, :], in0=t1[:, :], in1=ps[pv][:64, :129], op=ALU.add), [("mt1" + H), PSN[pv]], [("mtot" + H)])
                            yield
                            dve(lambda e: e.tensor_scalar(out=gh[:, 4:5], in0=tot[:, 128:129], scalar1=-1.0, scalar2=None, op0=ALU.mult), [("mtot" + H)], [MG])
                            yield
                            dve(lambda e: e.tensor_tensor(out=gh[:, 4:5], in0=gh[:, 4:5], in1=tot[:, 128:129], op=ALU.max), [("mtot" + H), "mgt", MG], [MG])
                            yield
                            act(gh[:, 5:6], gh[:, 2:3], AF.Exp, ["mgt", MG], [MG])
                            yield
                            dve(lambda e: e.tensor_tensor(out=gh[:, 4:5], in0=gh[:, 4:5], in1=gh[:, 5:6], op=ALU.max), ["mgt", MG], [MG])
                            yield
                            dve(lambda e: e.reciprocal(out=gh[:, 4:5], in_=gh[:, 4:5]), ["mgt", MG], [MG])
                            yield
                            dve(lambda e: e.tensor_scalar(out=hs[:, :], in0=tot[:, 0:128], scalar1=gh[:, 4:5], scalar2=None, op0=ALU.mult), [("mtot" + H), "mgt", MG], [("mhs" + H)])
                            yield
                            pm_ = nextps()
                            mm(pm_, ps[pm_][:, 0:1], SELL[:, :], gh[:, 0:1], True, True, ["mgt", MG, "SELL"])
                            yield
                            dve(lambda e: e.tensor_copy(out=g1[:, 0:1], in_=ps[pm_][:, 0:1]), [PSN[pm_]], [M1])
                            yield
                            dve(lambda e: e.tensor_tensor(out=g1[:, 1:2], in0=g128[:, h:h + 1], in1=mb[:, h:h + 1], op=ALU.add), ["mg128", M1, ("mmb" + H)], [M1])
                            yield
                            dve(lambda e: e.tensor_tensor(out=g1[:, 1:2], in0=g1[:, 1:2], in1=g1[:, 0:1], op=ALU.subtract), ["mg128", M1], [M1])
                            yield
                            act(g1[:, 1:2], g1[:, 1:2], AF.Exp, ["mg128", M1], [M1])
                            yield
                            dve(lambda e: e.tensor_tensor(out=gh[:, 3:4], in0=g128[:64, h:h + 1], in1=gt[:, 12 + h:13 + h], op=ALU.subtract), ["mg128", M1, "mgt", MG], [MG])
                            yield
                            dve(lambda e: e.tensor_tensor(out=gh[:, 3:4], in0=gh[:, 3:4], in1=g1[:64, 0:1], op=ALU.subtract), ["mg128", M1, "mgt", MG], [MG])
                            yield
                            act(gh[:, 3:4], gh[:, 3:4], AF.Exp, ["mgt", MG], [MG])
                            yield
                            dve(lambda e: e.tensor_scalar(out=kw[:, :], in0=kk, scalar1=gh[:, 3:4], scalar2=SC, op0=ALU.mult, op1=ALU.mult), ["mptc", "mgt", MG], [("mkw" + H)])
                            yield
                            pU = nextps()
                            mm(pU, ps[pU][:, :129], kw[:, :], v1[:, h, :], True, True, [("mkw" + H), "mv1"])
                            yield
                            dve(lambda e: e.scalar_tensor_tensor(out=Cn[:, h, :], in0=Cn[:, h, :], scalar=g1[:, 1:2], in1=ps[pU][:, :129], op0=ALU.mult, op1=ALU.add), [("mCn" + H), "mg128", M1, PSN[pU]], [("mCn" + H)])
                            yield
                            dve(lambda e: e.tensor_copy(out=mb[:, h:h + 1], in_=g1[:, 0:1]), ["mg128", M1], [("mmb" + H)])
                            yield
                            dve(lambda e: e.tensor_tensor(out=hsq[:, :], in0=hs[:, :], in1=hs[:, :], op=ALU.mult), [("mhs" + H)], [("mhsq" + H)])
                            yield
                            dve(lambda e: e.tensor_reduce(out=gh[:, 7:8], in_=hsq[:, :], axis=AX.X, op=ALU.add), [("mhsq" + H)], [MG])
                            yield
                            small_rstd(gh[:, 7:8], 128, MG)
                            yield
                            dve(lambda e: e.scalar_tensor_tensor(out=hs[:, :], in0=hs[:, :], scalar=gh[:, 7:8], in1=mng[:64, :], op0=ALU.mult, op1=ALU.mult), [("mhs" + H), "mgt", MG, "par"], [("mhs" + H)])
                            yield
                            dve(lambda e: e.tensor_tensor(out=hmix[:, 128 * h:128 * h + 128], in0=hs[:, :], in1=sg[:, 128 * h:128 * h + 128], op=ALU.mult), [("mhs" + H), "msg"], ["mhmix"])
                            yield
                        gens = [mhead(h) for h in range(4)]
                        while gens:
                            for gen in list(gens):
                                try:
                                    next(gen)
                                except StopIteration:
                                    gens.remove(gen)
                        pm = nextps()
                        for i in range(4):
                            tr(pm, ps[pm][:, i * 64:(i + 1) * 64], hmix[:, i * 128:(i + 1) * 128], 64, ["mhmix"])
                        act(mst[:, :, :], ps[pm][:, 0:256].rearrange("p (c t) -> p c t", t=64), AF.Copy, [PSN[pm]], ["mmst"])
                        stg(mixT[g].rearrange("(c p) t -> p c t", p=128)[:, 4:8, r0:r0 + 64], mst[:, :, :], ["mmst"])
                stg(o_mc[g][l].rearrange("h k v -> k h v"), Cn[:, :, 0:128], MCN)
                stg(o_mn[g][l].rearrange("h (k o) -> k h o", o=1), Cn[:, :, 128:129], MCN, allow_slow_non_contiguous=True)
                stg(o_mm[g][l].rearrange("(o h) -> o h", o=1), mb[0:1, :], MMBN)
                S.barrier()

        def phase_C1(l):
            with contextlib.ExitStack() as ph:
                Wo = sb(ph, "Wo", [128, 8, D], BF16)
                for c in range(8):
                    ld(Wo[:, c, :], w_out[l, c * 128:(c + 1) * 128, :], ["Wo"], q="pool")
                mt = sb(ph, "c1m", [128, 8, 512], BF16); yT = sb(ph, "c1y", [128, 8, 512])
                sq = sb(ph, "c1sq", [128, 8, 512]); rstd = sb(ph, "c1r", [128, 512]); xt = sb(ph, "c1x", [128, 8, 512])
                for g in "ps":
                    for (t0, TT) in tiles_of(TG[g], 512):
                        ld(mt[:, :, :TT], mixT[g].rearrange("(c p) t -> p c t", p=128)[:, :, t0:t0 + TT], ["c1m"])
                        ld(xt[:, :, :TT], xT[g].rearrange("(c p) t -> p c t", p=128)[:, :, t0:t0 + TT], ["c1x"])
                        for ob in range(8):
                            pi = nextps()
                            for c in range(8):
                                mm(pi, ps[pi][:, :TT], Wo[:, c, ob * 128:(ob + 1) * 128], mt[:, c, :TT], c == 0, c == 7, ["Wo", "c1m"])
                            act(yT[:, ob, :TT], ps[pi][:, :TT], AF.Copy, [PSN[pi]], ["c1y"])
                        fm_rstd(yT, sq, rstd, TT, "c1y", "c1sq", "c1r")
                        for ob in range(8):
                            dve(lambda e: e.scalar_tensor_tensor(out=yT[:, ob, :TT], in0=yT[:, ob, :TT], scalar=gpost[:, ob:ob + 1], in1=rstd[:, :TT], op0=ALU.mult, op1=ALU.mult), ["c1y", "c1r", "par"], ["c1y"])
                        pool(lambda e: e.tensor_tensor(out=xt[:, :, :TT], in0=xt[:, :, :TT], in1=yT[:, :, :TT], op=ALU.add), ["c1x", "c1y"], ["c1x"])
                        stg(x1T[g].rearrange("(c p) t -> p c t", p=128)[:, :, t0:t0 + TT], xt[:, :, :TT], ["c1x"])
                S.barrier()

        def phase_C2(l, hf, last):
            HB = 11
            c0 = hf * HB * 128
            with contextlib.ExitStack() as ph:
                Wg = sb(ph, "Wg", [128, 8, HB * 128], BF16); Wu = sb(ph, "Wu", [128, 8, HB * 128], BF16)
                Wd = sb(ph, "Wd", [128, HB, D], BF16)
                for c in range(8):
                    ld(Wg[:, c, :], ffn_w_up[l, c * 128:(c + 1) * 128, c0:c0 + HB * 128], ["Wg"], q="pool")
                    ld(Wu[:, c, :], ffn_w_up[l, c * 128:(c + 1) * 128, DFF + c0:DFF + c0 + HB * 128], ["Wu"], q="pool")
                for fb in range(HB):
                    ld(Wd[:, fb, :], ffn_w_down[l, c0 + fb * 128:c0 + (fb + 1) * 128, :], ["Wd"], q="pool")
                x1 = sb(ph, "c2x", [128, 8, 512]); sq = sb(ph, "c2sq", [128, 8, 512]); rstd = sb(ph, "c2r", [128, 512])
                hT = sb(ph, "c2h", [128, 8, 512], BF16)
                G = [sb(ph, "c2G%d" % i, [128, 2 + 512]) for i in range(2)]
                halo = sb(ph, "c2halo", [128, HB, 2])
                cv = [sb(ph, "c2cv%d" % i, [128, 512]) for i in range(2)]
                aT = sb(ph, "c2a", [128, HB, 512], BF16)
                y2 = sb(ph, "c2y", [128, 8, 512])
                yo = sb(ph, "c2yo", [128, D]) if (last and hf == 1) else None
                for g in "ps":
                    T = TG[g]
                    if g == "s":
                        for j in range(2):
                            ld(halo[:, :, j], st_fconv[l, j].rearrange("(b p) -> p b", p=128)[:, HB * hf:HB * hf + HB], ["c2halo"], allow_slow_non_contiguous=True)
                    else:
                        dve(lambda e: e.memset(halo[:], 0.0), [], ["c2halo"])
                    tl = tiles_of(T, 512)
                    for ti, (t0, TT) in enumerate(tl):
                        ld(x1[:, :, :TT], x1T[g].rearrange("(c p) t -> p c t", p=128)[:, :, t0:t0 + TT], ["c2x"])
                        fm_rstd(x1, sq, rstd, TT, "c2x", "c2sq", "c2r")
                        for c in range(8):
                            dve(lambda e: e.scalar_tensor_tensor(out=hT[:, c, :TT], in0=x1[:, c, :TT], scalar=gfpre[:, c:c + 1], in1=rstd[:, :TT], op0=ALU.mult, op1=ALU.mult), ["c2x", "c2r", "par"], ["c2h"])
                        if hf == 1:
                            ld(sq[:, :, :TT], y2p[g].rearrange("(c p) t -> p c t", p=128)[:, :, t0:t0 + TT], ["c2sq"])
                        for fb in range(HB):
                            b2 = fb % 2
                            Gb = G[b2]; Gn = "c2G%d" % b2; cvb = cv[b2]; cn = "c2cv%d" % b2
                            fbg = HB * hf + fb
                            pg = nextps(); pu = nextps()
                            for c in range(8):
                                mm(pg, ps[pg][:, :TT], Wg[:, c, fb * 128:(fb + 1) * 128], hT[:, c, :TT], c == 0, c == 7, ["Wg", "c2h"])
                            for c in range(8):
                                mm(pu, ps[pu][:, :TT], Wu[:, c, fb * 128:(fb + 1) * 128], hT[:, c, :TT], c == 0, c == 7, ["Wu", "c2h"])
                            act(Gb[:, 2:2 + TT], ps[pg][:, :TT], AF.Copy, [PSN[pg]], [Gn])
                            pool(lambda e: e.tensor_copy(out=Gb[:, 0:2], in_=halo[:, fb, :]), ["c2halo"], [Gn])
                            pool(lambda e: e.tensor_copy(out=halo[:, fb, :], in_=Gb[:, TT:TT + 2]), [Gn], ["c2halo"])
                            dve(lambda e: e.tensor_scalar(out=cvb[:, :TT], in0=Gb[:, 0:TT], scalar1=wcf[:, fbg, 0:1], scalar2=None, op0=ALU.mult), [Gn, "par"], [cn])
                            for j in (1, 2):
                                dve(lambda e: e.scalar_tensor_tensor(out=cvb[:, :TT], in0=Gb[:, j:j + TT], scalar=wcf[:, fbg, j:j + 1], in1=cvb[:, :TT], op0=ALU.mult, op1=ALU.add), [Gn, "par", cn], [cn])
                            act(cvb[:, :TT], cvb[:, :TT], AF.Gelu_apprx_tanh, [cn], [cn])
                            dve(lambda e: e.tensor_tensor(out=aT[:, fb, :TT], in0=cvb[:, :TT], in1=ps[pu][:, :TT], op=ALU.mult), [cn, PSN[pu]], ["c2a"])
                        if ti == len(tl) - 1:
                            for j in range(2):
                                stg(o_fconv[g][l, j].rearrange("(b p) -> p b", p=128)[:, HB * hf:HB * hf + HB], halo[:, :, j], ["c2halo"], allow_slow_non_contiguous=True)
                        for ob in range(8):
                            pi = nextps()
                            for fb in range(HB):
                                mm(pi, ps[pi][:, :TT], Wd[:, fb, ob * 128:(ob + 1) * 128], aT[:, fb, :TT], fb == 0, fb == HB - 1, ["Wd", "c2a"])
                            if hf == 0:
                                act(y2[:, ob, :TT], ps[pi][:, :TT], AF.Copy, [PSN[pi]], ["c2y"])
                            else:
                                dve(lambda e: e.tensor_tensor(out=y2[:, ob, :TT], in0=sq[:, ob, :TT], in1=ps[pi][:, :TT], op=ALU.add), ["c2sq", PSN[pi]], ["c2y"])
                        if hf == 0:
                            stg(y2p[g].rearrange("(c p) t -> p c t", p=128)[:, :, t0:t0 + TT], y2[:, :, :TT], ["c2y"])
                            continue
                        fm_rstd(y2, sq, rstd, TT, "c2y", "c2sq", "c2r")
                        for ob in range(8):
                            dve(lambda e: e.scalar_tensor_tensor(out=y2[:, ob, :TT], in0=y2[:, ob, :TT], scalar=gfpost[:, ob:ob + 1], in1=rstd[:, :TT], op0=ALU.mult, op1=ALU.mult), ["c2y", "c2r", "par"], ["c2y"])
                        pool(lambda e: e.tensor_tensor(out=x1[:, :, :TT], in0=x1[:, :, :TT], in1=y2[:, :, :TT], op=ALU.add), ["c2x", "c2y"], ["c2x"])
                        if not last:
                            stg(xT[g].rearrange("(c p) t -> p c t", p=128)[:, :, t0:t0 + TT], x1[:, :, :TT], ["c2x"])
                        else:
                            NT = min(128, TT)
                            for j in range(TT // NT):
                                pa = nextps(); pb = nextps()
                                for c in range(8):
                                    pi = pa if c < 4 else pb
                                    tr(pi, ps[pi][:NT, (c % 4) * 128:(c % 4 + 1) * 128], x1[:, c, j * NT:(j + 1) * NT], 128, ["c2x"])
                                act(yo[:NT, 0:512], ps[pa][:NT, :], AF.Copy, [PSN[pa]], ["c2yo"])
                                dve(lambda e: e.tensor_copy(out=yo[:NT, 512:1024], in_=ps[pb][:NT, :]), [PSN[pb]], ["c2yo"])
                                stg(y_out[g][t0 + j * NT:t0 + (j + 1) * NT, :], yo[:NT, :], ["c2yo"])
                S.barrier()

        steps = [phase_T0]
        for l in range(2):
            steps.append(lambda l=l: load_params(l))
            steps.append(lambda l=l: phase_A(l))
            for g in "ps":
                steps.append(lambda l=l, g=g: phase_attn(l, g))
                steps.append(lambda l=l, g=g: phase_gdn(l, g))
                steps.append(lambda l=l, g=g: phase_mlstm(l, g))
            steps.append(lambda l=l: phase_C1(l))
            steps.append(lambda l=l: phase_C2(l, 0, l == 1))
            steps.append(lambda l=l: phase_C2(l, 1, l == 1))
        for i, stp in enumerate(steps):
            if i < KLIM:
                stp()
        S.barrier()
        n_inst = S.n_inst
    return nc, n_inst


_CACHE = {}


def _level_masks():
    m = np.zeros((12, 64, 64), np.float32)
    i = np.arange(64)
    for lv in range(6):
        b = 1 << lv
        same = (i[:, None] // (2 * b)) == (i[None, :] // (2 * b))
        first = (i % (2 * b)) < b
        mu = same & first[:, None] & (~first)[None, :]
        m[2 * lv] = mu
        m[2 * lv + 1] = mu.T
    return m


def _run(inputs, SEQ, PAST):
    key = (SEQ, PAST)
    if key not in _CACHE:
        _CACHE[key] = build(SEQ, PAST)
    nc, _ = _CACHE[key]
    f = lambda a: np.ascontiguousarray(np.asarray(a, dtype=np.float32))
    shared = ["g_mix_pre", "g_mix_post", "g_ffn_pre", "g_ffn_post", "w_in", "gdn_conv_w", "gdn_a_log", "gdn_dt_bias",
              "gdn_norm_g", "mlstm_b_i", "mlstm_b_f", "mlstm_norm_g", "w_out", "ffn_w_up", "ffn_conv_w", "ffn_w_down"]
    percore = ["cache_sb_k", "cache_sb_v", "state_gdn_conv", "state_gdn_s", "state_mlstm_c", "state_mlstm_n",
               "state_mlstm_m", "state_ffn_conv"]
    base = {k: f(inputs[k]) for k in shared}
    base["cmask"] = _level_masks()
    base["x_prompt"] = f(inputs["x_prompt"])[0]
    in_maps = []
    for c in range(8):
        m = dict(base)
        m["x_sample"] = f(inputs["x_sample"])[c]
        for k in percore:
            m[k] = f(np.asarray(inputs[k])[:, c])
        in_maps.append(m)
    res = run_bass_kernel_spmd(nc, in_maps, core_ids=list(range(8)))
    R = res.results
    outs = [R[0]["y_prompt"][None], np.stack([R[c]["y_sample"] for c in range(8)])]
    for nm in ("sb_k", "sb_v", "gdn_conv", "gdn_s", "mlstm_c", "mlstm_n", "mlstm_m", "ffn_conv"):
        outs.append(np.asarray(R[0][nm + "_p"])[:, None])
    for nm in ("sb_k", "sb_v", "gdn_conv", "gdn_s", "mlstm_c", "mlstm_n", "mlstm_m", "ffn_conv"):
        outs.append(np.stack([np.asarray(R[c][nm + "_s"]) for c in range(8)], axis=1))
    return tuple(np.ascontiguousarray(o, dtype=np.float32) for o in outs)


def kernel(**inputs):
    SEQ = int(np.asarray(inputs["x_prompt"]).shape[1])
    PAST = int(np.asarray(inputs["cache_sb_k"]).shape[3])
    return _run(inputs, SEQ, PAST)
```
